# Optimizing a Trainium2 kernel written in Bass

```python
import math
import jax
import jax.numpy as jnp
from jax import lax
import numpy as np

D_MODEL = 1024
BATCH = 2
SEQ = 8192
DEPTH = 2
DEC_BATCH = 1
DEC_SEQ = 16384
PAST_LEN = 128

HEAD_DIM = 64
POOL_WIDTH = D_MODEL // 4
POOL_WINDOWS = (2, 4, 8, 16)
N_POOL_GROUPS = len(POOL_WINDOWS)
POOL_GROUP = POOL_WIDTH // N_POOL_GROUPS
RWKV_WIDTH = 3 * D_MODEL // 8
RWKV_HEADS = RWKV_WIDTH // HEAD_DIM
DECAY_LORA = 64
ICLR_LORA = 64
GATE_LORA = 128
ATT_WIDTH = 3 * D_MODEL // 8
ATT_HEADS = ATT_WIDTH // HEAD_DIM
DILATED_CONFIGS = ((128, 1), (512, 4), (2048, 16))
REL_BUCKETS = 32
REL_MAX_DISTANCE = 1024
MIX_WIDTH = POOL_WIDTH + RWKV_WIDTH + ATT_WIDTH
SPLIT_SIZES = (POOL_WIDTH, 3 * RWKV_WIDTH, DECAY_LORA, ICLR_LORA, GATE_LORA, 3 * ATT_WIDTH)
IN_WIDTH = sum(SPLIT_SIZES)
D_FF = 4 * D_MODEL
NORM_EPS = 1e-6
GN_EPS = 64e-5
NEG_INF = -1e30

kernel_name = 'hybrid_pool_rwkv7_dilated_encoder'


def rms_norm(x, gain):
    xf = x.astype(jnp.float32)
    y = xf * lax.rsqrt(jnp.mean(xf * xf, axis=-1, keepdims=True) + NORM_EPS)
    return (y * gain.astype(jnp.float32)).astype(x.dtype)


def split_columns(proj):
    parts, start = [], 0
    for size in SPLIT_SIZES:
        parts.append(proj[..., start:start + size])
        start += size
    return parts


def pool_mixer(u, pool_w, pool_scale):
    B, T, _ = u.shape
    uf = u.astype(jnp.float32)
    csum = jnp.concatenate([jnp.zeros((B, 1, POOL_WIDTH), jnp.float32), jnp.cumsum(uf, axis=1)], axis=1)
    pos = jnp.arange(T)
    pooled = []
    for g, win in enumerate(POOL_WINDOWS):
        lo = jnp.clip(pos - win // 2, 0, T)
        hi = jnp.clip(pos + win - win // 2, 0, T)
        cols = slice(g * POOL_GROUP, (g + 1) * POOL_GROUP)
        c = csum[:, :, cols]
        mean = (c[:, hi] - c[:, lo]) / (hi - lo).astype(jnp.float32)[None, :, None]
        pooled.append(mean - uf[:, :, cols])
    p = jnp.stack(pooled, axis=2)
    y = jnp.einsum('btgc,gcd->btgd', p, pool_w.astype(jnp.float32)).reshape(B, T, POOL_WIDTH)
    return (y * pool_scale.astype(jnp.float32)).astype(u.dtype)


def centred_short_conv(x, w):
    xp = jnp.pad(x, ((0, 0), (1, 1), (0, 0)))
    return xp[:, :-2] * w[0] + xp[:, 1:-1] * w[1] + xp[:, 2:] * w[2]


def rwkv7_step(S, inp):
    r, decay, k, v, a, b = inp
    sa = jnp.einsum('dbhvk,dbhk->dbhv', S, a)
    S = S * decay[..., None, :] + sa[..., :, None] * b[..., None, :] + v[..., :, None] * k[..., None, :]
    return S, jnp.einsum('dbhvk,dbhk->dbhv', S, r)


def bidirectional_rwkv7(rkv, w_lat, a_lat, g_lat, decay_w0, decay_up, iclr_a0, iclr_up, gate_up,
                        key_k, key_a, bonus_rk, gn_gain, gn_bias):
    B, T, _ = rkv.shape
    f32 = jnp.float32
    H, N = RWKV_HEADS, HEAD_DIM
    heads = lambda z: z.reshape(z.shape[:-1] + (H, N))
    r, k, v = jnp.split(rkv.astype(f32), 3, axis=-1)
    w = decay_w0.astype(f32)[:, None, None, :] + jnp.einsum('btr,drc->dbtc', jnp.tanh(w_lat.astype(f32)), decay_up.astype(f32))
    decay = jnp.exp(-jnp.exp(-jax.nn.softplus(-w) - 0.5))
    a = jax.nn.sigmoid(iclr_a0.astype(f32)[:, None, None, :]
                       + jnp.einsum('btr,drc->dbtc', a_lat.astype(f32), iclr_up.astype(f32)))
    g = jnp.einsum('btr,rc->btc', jax.nn.sigmoid(g_lat.astype(f32)), gate_up.astype(f32))
    kk = heads(k * key_k.astype(f32))
    kk = kk / jnp.maximum(jnp.sqrt(jnp.sum(kk * kk, axis=-1, keepdims=True)), 1e-12)
    k_eff = heads(k[None] * (1.0 + (a - 1.0) * key_a.astype(f32)))
    a_h = heads(a)
    shape = (2, B, T, H, N)
    r_h = jnp.broadcast_to(heads(r), shape)
    v_h = jnp.broadcast_to(heads(v), shape)
    kk_h = jnp.broadcast_to(kk, shape)

    def time_major(z):
        z = jnp.stack([z[0], jnp.flip(z[1], axis=1)])
        return jnp.moveaxis(z, 2, 0)

    xs = tuple(time_major(z) for z in (r_h, heads(decay), k_eff, v_h, -kk_h, kk_h * a_h))
    _, ys = lax.scan(rwkv7_step, jnp.zeros((2, B, H, N, N), f32), xs)
    ys = jnp.moveaxis(ys, 0, 2)
    y = ys[0] + jnp.flip(ys[1], axis=1)
    mu = jnp.mean(y, axis=-1, keepdims=True)
    var = jnp.mean(jnp.square(y - mu), axis=-1, keepdims=True)
    y = ((y - mu) * lax.rsqrt(var + GN_EPS)).reshape(B, T, RWKV_WIDTH) * gn_gain.astype(f32) + gn_bias.astype(f32)
    bonus = jnp.sum(r_h * k_eff * heads(bonus_rk.astype(f32)), axis=-1, keepdims=True) * v_h
    y = y + jnp.sum(bonus, axis=0).reshape(B, T, RWKV_WIDTH)
    return (y * g).astype(rkv.dtype)


def t5_relative_bucket(rel):
    nb = REL_BUCKETS // 2
    max_exact = nb // 2
    ret = (rel > 0).astype(jnp.int32) * nb
    n = jnp.abs(rel)
    nf = jnp.maximum(n, max_exact).astype(jnp.float32)
    large = max_exact + (jnp.log(nf / max_exact) / math.log(REL_MAX_DISTANCE / max_exact) * (nb - max_exact)).astype(jnp.int32)
    large = jnp.minimum(large, nb - 1)
    return ret + jnp.where(n < max_exact, n, large)


def dilated_branch(q, k, v, rel_bias, window, dilation):
    B, T, H, Dh = q.shape
    half = window // (2 * dilation)
    L = T // dilation
    nb = -(-L // half)
    Lp = nb * half

    def strided(z):
        z = z.reshape(B, L, dilation, H, Dh).transpose(0, 2, 1, 3, 4)
        return jnp.pad(z, ((0, 0), (0, 0), (0, Lp - L), (0, 0), (0, 0)))

    def band(z):
        zp = jnp.pad(z, ((0, 0), (0, 0), (half, half), (0, 0), (0, 0)))
        parts = [zp[:, :, s:s + Lp].reshape(B, dilation, nb, half, H, Dh) for s in (0, half, 2 * half)]
        return jnp.concatenate(parts, axis=3)

    qb = strided(q).reshape(B, dilation, nb, half, H, Dh)
    kb = band(strided(k))
    vb = band(strided(v))
    qi = jnp.arange(half)[:, None]
    kj = jnp.arange(3 * half)[None, :]
    rel = kj - half - qi
    kpos = jnp.arange(nb)[:, None, None] * half + kj[None] - half
    valid = (jnp.abs(rel) <= half)[None] & (kpos >= 0) & (kpos < L)
    bias = jnp.transpose(rel_bias.astype(jnp.float32)[t5_relative_bucket(rel * dilation)], (2, 0, 1))
    logits = jnp.einsum('brnqhd,brnkhd->brnhqk', qb, kb) * (Dh ** -0.5) + bias
    logits = jnp.where(valid[None, None, :, None], logits, NEG_INF)
    m = jnp.max(logits, axis=-1, keepdims=True)
    p = jnp.exp(logits - m)
    s = jnp.sum(p, axis=-1, keepdims=True)
    o = jnp.einsum('brnhqk,brnkhd->brnqhd', p, vb) / jnp.swapaxes(s, 3, 4)
    lse = jnp.swapaxes((m + jnp.log(s))[..., 0], 3, 4)
    o = o.reshape(B, dilation, Lp, H, Dh)[:, :, :L].transpose(0, 2, 1, 3, 4).reshape(B, T, H, Dh)
    lse = lse.reshape(B, dilation, Lp, H)[:, :, :L].transpose(0, 2, 1, 3).reshape(B, T, H)
    return o, lse


def dilated_attention(q, k, v, rel_bias):
    outs, lses = [], []
    for window, dilation in DILATED_CONFIGS:
        o, lse = dilated_branch(q, k, v, rel_bias, window, dilation)
        outs.append(o)
        lses.append(lse)
    alpha = jax.nn.softmax(jnp.stack(lses), axis=0)
    return jnp.sum(alpha[..., None] * jnp.stack(outs), axis=0)


def encoder_layer(x, layer_w, rel_bias):
    (n_mix_pre, n_mix_post, n_ffn_pre, n_ffn_post, w_in, w_out, pool_w, pool_scale, rwkv_conv,
     decay_w0, decay_up, iclr_a0, iclr_up, gate_up, key_k, key_a, bonus_rk, gn_gain, gn_bias,
     w_ff_in, w_ff_out) = layer_w
    B, T, _ = x.shape
    h = rms_norm(x, n_mix_pre)
    u_pool, rkv, w_lat, a_lat, g_lat, qkv = split_columns(h @ w_in)
    y_pool = pool_mixer(u_pool, pool_w, pool_scale)
    y_rwkv = bidirectional_rwkv7(centred_short_conv(rkv, rwkv_conv), w_lat, a_lat, g_lat, decay_w0, decay_up,
                                 iclr_a0, iclr_up, gate_up, key_k, key_a, bonus_rk, gn_gain, gn_bias)
    q, k, v = [z.astype(jnp.float32).reshape(B, T, ATT_HEADS, HEAD_DIM) for z in jnp.split(qkv, 3, axis=-1)]
    y_att = dilated_attention(q, k, v, rel_bias).reshape(B, T, ATT_WIDTH).astype(x.dtype)
    mixed = jnp.concatenate([y_pool, y_rwkv, y_att], axis=-1) @ w_out
    x = x + rms_norm(mixed, n_mix_post)
    h = rms_norm(x, n_ffn_pre)
    f = jnp.square(jax.nn.relu(h @ w_ff_in)) @ w_ff_out
    return x + rms_norm(f, n_ffn_post)


def setup_inputs(seed: int = 0) -> dict:
    key = jax.random.key(seed)
    ks = jax.random.split(key, 24)
    f32 = jnp.float32
    nrm = lambda k, shape, scale: scale * jax.random.normal(k, shape, f32)
    gain = lambda k, shape: 1.0 + 0.05 * jax.random.normal(k, shape, f32)
    RW = RWKV_WIDTH
    conv_base = jnp.array([0.25, 1.0, 0.25], f32)[None, :, None]
    return {
        'x_prompt': nrm(ks[0], (BATCH, SEQ, D_MODEL), 1.0),
        'x_sample': nrm(ks[1], (DEC_BATCH, DEC_SEQ, D_MODEL), 1.0),
        'norm_mix_pre': gain(ks[2], (DEPTH, D_MODEL)),
        'norm_mix_post': gain(ks[3], (DEPTH, D_MODEL)),
        'norm_ffn_pre': gain(ks[4], (DEPTH, D_MODEL)),
        'norm_ffn_post': gain(ks[5], (DEPTH, D_MODEL)),
        'w_in': nrm(ks[6], (DEPTH, D_MODEL, IN_WIDTH), D_MODEL ** -0.5),
        'w_out': nrm(ks[7], (DEPTH, MIX_WIDTH, D_MODEL), MIX_WIDTH ** -0.5),
        'pool_w': nrm(ks[8], (DEPTH, N_POOL_GROUPS, POOL_GROUP, POOL_GROUP), POOL_GROUP ** -0.5),
        'pool_scale': gain(ks[9], (DEPTH, POOL_WIDTH)),
        'rwkv_conv': conv_base + nrm(ks[10], (DEPTH, 3, 3 * RW), 0.05),
        'decay_w0': jax.random.uniform(ks[11], (DEPTH, 2, RW), f32, -6.0, -1.0),
        'decay_up': nrm(ks[12], (DEPTH, 2, DECAY_LORA, RW), 0.5 * DECAY_LORA ** -0.5),
        'iclr_a0': nrm(ks[13], (DEPTH, 2, RW), 0.5),
        'iclr_up': nrm(ks[14], (DEPTH, 2, ICLR_LORA, RW), 0.5 * ICLR_LORA ** -0.5),
        'gate_up': nrm(ks[15], (DEPTH, GATE_LORA, RW), GATE_LORA ** -0.5),
        'key_k': 0.85 + nrm(ks[16], (DEPTH, RW), 0.05),
        'key_a': gain(ks[17], (DEPTH, RW)),
        'bonus_rk': nrm(ks[18], (DEPTH, RW), 0.1),
        'gn_gain': gain(ks[19], (DEPTH, RW)),
        'gn_bias': nrm(ks[20], (DEPTH, RW), 0.02),
        'rel_bias': nrm(ks[21], (REL_BUCKETS, ATT_HEADS), 0.5),
        'w_ff_in': nrm(ks[22], (DEPTH, D_MODEL, D_FF), D_MODEL ** -0.5),
        'w_ff_out': nrm(ks[23], (DEPTH, D_FF, D_MODEL), D_FF ** -0.5),
    }


def reference(x_prompt, x_sample, norm_mix_pre, norm_mix_post, norm_ffn_pre, norm_ffn_post, w_in, w_out,
              pool_w, pool_scale, rwkv_conv, decay_w0, decay_up, iclr_a0, iclr_up, gate_up, key_k, key_a,
              bonus_rk, gn_gain, gn_bias, rel_bias, w_ff_in, w_ff_out):
    layer_stacks = (norm_mix_pre, norm_mix_post, norm_ffn_pre, norm_ffn_post, w_in, w_out, pool_w, pool_scale,
                    rwkv_conv, decay_w0, decay_up, iclr_a0, iclr_up, gate_up, key_k, key_a, bonus_rk,
                    gn_gain, gn_bias, w_ff_in, w_ff_out)

    def run_trunk(x):
        for l in range(DEPTH):
            x = encoder_layer(x, [w[l] for w in layer_stacks], rel_bias)
        return x

    y_prompt = run_trunk(x_prompt)
    y_sample = run_trunk(x_sample)
    return (y_prompt, y_sample)
```

```python
import numpy as np
import concourse.bass as bass
import concourse.mybir as mybir
from contextlib import ExitStack

F32 = mybir.dt.float32
BF16 = mybir.dt.bfloat16
I32 = mybir.dt.int32
ALU = mybir.AluOpType
AF = mybir.ActivationFunctionType
AX = mybir.AxisListType

SEM_LIMIT = 30000
NDMASEM = 8


class Buf:
    __slots__ = ("t", "name", "last_w", "readers")

    def __init__(self, t, name):
        self.t = t
        self.name = name
        self.last_w = None
        self.readers = []

    def __getitem__(self, idx):
        return self.t[idx]


class Prog:
    ENG = ("pe", "act", "dve", "pool", "sp")

    def __init__(self, nc, es):
        self.nc = nc
        self.es = es
        self.streams = {e: [] for e in self.ENG}
        self.cnt = {e: 0 for e in self.ENG}
        self.sem = {}
        self.semlist = {e: [] for e in self.ENG}
        self.waited = {e: {} for e in self.ENG}
        self.dma_sems = {}
        self.dma_n = {}
        for e in ("sp", "pool", "act"):
            self.dma_sems[e] = [es.enter_context(nc.semaphore(f"dq_{e}_{i}")) for i in range(NDMASEM)]
            self.dma_n[e] = 0
        for e in ("pe", "act", "dve", "pool"):
            self._newsem(e)
        self.nbuf = 0

    def _newsem(self, e):
        s = self.es.enter_context(self.nc.semaphore(f"s_{e}_{len(self.semlist[e])}"))
        self.semlist[e].append(s)
        self.sem[e] = s
        self.cnt[e] = 0

    def sb(self, shape, dt, name=None):
        self.nbuf += 1
        name = name or f"sb{self.nbuf}"
        t = self.es.enter_context(self.nc.sbuf_tensor(f"{name}_{self.nbuf}", list(shape), dt))
        return Buf(t, name)

    def ps(self, shape, dt=F32, name=None):
        self.nbuf += 1
        name = name or f"ps{self.nbuf}"
        t = self.es.enter_context(self.nc.psum_tensor(f"{name}_{self.nbuf}", list(shape), dt))
        return Buf(t, name)

    def ps_multi(self, n, shape, dt=F32, name=None):
        self.nbuf += 1
        name = name or f"psm{self.nbuf}"
        t = self.es.enter_context(self.nc.psum_tensor(f"{name}_{self.nbuf}", [shape[0], n, shape[1]], dt))
        return [Buf(t[:, i, :], f"{name}{i}") for i in range(n)]

    def dram(self, name, shape, dt, kind="Internal", addr_space="Local"):
        t = self.nc.dram_tensor(name, list(shape), dt, kind=kind, addr_space=addr_space)
        return Buf(t.ap(), name)

    def _wait(self, eng, ev):
        if ev is None:
            return
        sem, val, src = ev
        if src == "pe" and eng == "pe":
            return
        w = self.waited[eng]
        key = id(sem)
        if w.get(key, 0) >= val:
            return
        w[key] = val
        self.streams[eng].append(("wait", sem, val))

    def _deps(self, eng, reads, writes):
        for b in reads:
            self._wait(eng, b.last_w)
        for b in writes:
            self._wait(eng, b.last_w)
            for ev in b.readers:
                self._wait(eng, ev)

    def _mark(self, ev, reads, writes):
        for b in writes:
            b.last_w = ev
            b.readers = []
        for b in reads:
            if b in writes:
                continue
            b.readers = [r for r in b.readers if not (r[0] is ev[0])] + [ev]

    def op(self, eng, fn, reads=(), writes=()):
        self._deps(eng, reads, writes)
        if self.cnt[eng] >= SEM_LIMIT:
            self._newsem(eng)
        self.cnt[eng] += 1
        ev = (self.sem[eng], self.cnt[eng], eng)
        self.streams[eng].append(("op", fn, self.sem[eng], 1))
        self._mark(ev, reads, writes)
        return ev

    def o(self, eng, name, reads, writes, *args, **kw):
        return self.op(eng, lambda e: getattr(e, name)(*args, **kw), reads, writes)

    def dma(self, q, out_ap, in_ap, reads=(), writes=(), **kw):
        n = self.dma_n[q]
        self.dma_n[q] += 1
        sem = self.dma_sems[q][n % NDMASEM]
        prev = 16 * (n // NDMASEM)
        if prev > 0:
            self._wait(q, (sem, prev, "dma"))
        self._deps(q, reads, writes)
        ev = (sem, prev + 16, "dma")
        self.streams[q].append(("op", lambda e: e.dma_start(out=out_ap, in_=in_ap, **kw), sem, 16))
        self._mark(ev, reads, writes)
        return ev

    def custom(self, q, fn, inc, reads=(), writes=()):
        n = self.dma_n[q]
        self.dma_n[q] += 1
        sem = self.dma_sems[q][n % NDMASEM]
        prev = 16 * (n // NDMASEM)
        if prev > 0:
            self._wait(q, (sem, prev, "dma"))
        self._deps(q, reads, writes)
        ev = (sem, prev + inc, "dma")
        assert inc == 16
        self.streams[q].append(("op", fn, sem, inc))
        self._mark(ev, reads, writes)
        return ev

    def barrier_all_dma(self, eng="sp"):
        for q in ("sp", "pool", "act"):
            n = self.dma_n[q]
            for i in range(NDMASEM):
                cnt = (n - i + NDMASEM - 1) // NDMASEM if n > i else 0
                if cnt > 0:
                    self._wait(eng, (self.dma_sems[q][i], 16 * cnt, "dma"))

    def phase_barrier(self):
        for eng in self.ENG:
            for f in ("pe", "act", "dve", "pool"):
                if self.cnt[f] > 0 and f != eng:
                    self._wait(eng, (self.sem[f], self.cnt[f], f))
            self.barrier_all_dma(eng)

    def emit(self):
        nc = self.nc
        streams = self.streams
        with nc.Block() as block:
            def run(e, lst):
                for it in lst:
                    if it[0] == "wait":
                        e.wait_ge(it[1], it[2])
                    else:
                        ins = it[1](e)
                        ins.then_inc(it[2], it[3])

            @block.tensor
            def _(e):
                run(e, streams["pe"])

            @block.scalar
            def _(e):
                run(e, streams["act"])

            @block.vector
            def _(e):
                run(e, streams["dve"])

            @block.gpsimd
            def _(e):
                run(e, streams["pool"])

            @block.sync
            def _(e):
                run(e, streams["sp"])


NT = 4096; HL = 1024; NE = NT + 2 * HL; TS = 512
D = 1024; INW = 2816
EPS = 1e-6


def rot(lst):
    st = {"i": 0}
    def nxt():
        b = lst[st["i"] % len(lst)]; st["i"] += 1
        return b
    return nxt


def rms_stats(P, xs, sq, ones_bf, ss_ps, tmp, rstd, eps_t, n_chunks=8, width=TS, dfeat=D):
    P.op("act", lambda e: e.activation(out=sq[:], in_=xs[:], func=AF.Square), reads=[xs], writes=[sq])
    for c in range(n_chunks):
        P.op("pe", lambda e, c=c: e.matmul(ss_ps[:], lhsT=ones_bf[:], rhs=sq[:, c, :], start=(c == 0), stop=(c == n_chunks - 1)),
             reads=[ones_bf, sq], writes=[ss_ps])
    P.op("act", lambda e: e.activation(out=tmp[:], in_=ss_ps[:], func=AF.Sqrt, scale=1.0 / dfeat, bias=eps_t[:]),
         reads=[ss_ps, eps_t], writes=[tmp])
    P.op("dve", lambda e: e.reciprocal(out=rstd[:], in_=tmp[:]), reads=[tmp], writes=[rstd])


def phase_A(P, es_outer, xT, gain_d, w_in_d, PT, Vtok, ntiles, hflag_d=None, cols=None, do_v=True, tiles=None):
    nc = P.nc
    with ExitStack() as es:
        P.es = es
        W = P.sb([128, 8, INW], BF16, "Win")
        gain = P.sb([128, 8], F32, "gain")
        ones_bf = P.sb([128, 128], BF16, "ones")
        xs2 = [P.sb([128, 8, TS], F32, f"xs{i}") for i in range(2)]
        sq = P.sb([128, 8, TS], BF16, "sq")
        hT = P.sb([128, 8, TS], BF16, "hT")
        tmp = P.sb([128, TS], F32, "tmp"); rstd = P.sb([128, TS], F32, "rstd")
        stg = rot([P.sb([128, TS], F32, f"stg{i}") for i in range(4)])
        vst = rot([P.sb([128, 384], BF16, f"vst{i}") for i in range(2)])
        ss_ps = P.ps([128, TS], F32, "ssps")
        mm = rot([P.ps([128, TS], F32, f"mm{i}") for i in range(4)])
        vps = rot([P.ps([128, 384], F32, f"vps{i}") for i in range(2)])
        for c in range(8):
            P.dma("pool", W[:, c, :], w_in_d[c, :, :], reads=[w_in_d], writes=[W])
        P.dma("sp", gain[:], gain_d[:], reads=[gain_d], writes=[gain])
        P.op("dve", lambda e: e.memset(ones_bf[:], 1.0), writes=[ones_bf])
        eps_t = P.sb([128, 1], F32, "eps")
        P.op("dve", lambda e: e.memset(eps_t[:], EPS), writes=[eps_t])
        if hflag_d is not None:
            hflag = P.sb([128, 2], F32, "hflag")
            P.dma("sp", hflag[:], hflag_d[:], reads=[hflag_d], writes=[hflag])
        def load(j):
            P.dma("sp", xs2[j % 2][:], xT[:, :, j * TS:(j + 1) * TS].rearrange("c p t -> p c t"), reads=[xT], writes=[xs2[j % 2]])
            if hflag_d is not None and (j < HL // TS or j >= ntiles - HL // TS):
                col = 0 if j < HL // TS else 1
                P.o("pool", "tensor_scalar", [xs2[j % 2], hflag], [xs2[j % 2]], out=xs2[j % 2][:], in0=xs2[j % 2][:], scalar1=hflag[:, col:col + 1], scalar2=0.0, op0=ALU.mult, op1=ALU.add)
        tile_list = list(range(ntiles)) if tiles is None else list(tiles)
        col_list = list(range(INW // 128)) if cols is None else list(cols)
        load(tile_list[0])
        evi = 0
        for ji, j in enumerate(tile_list):
            if ji + 1 < len(tile_list):
                load(tile_list[ji + 1])
            xs = xs2[j % 2]
            rms_stats(P, xs, sq, ones_bf, ss_ps, tmp, rstd, eps_t)
            for c in range(8):
                P.op("dve", lambda e, c=c, xs=xs: e.scalar_tensor_tensor(out=hT[:, c, :], in0=xs[:, c, :], scalar=gain[:, c:c + 1], in1=rstd[:], op0=ALU.mult, op1=ALU.mult),
                     reads=[xs, gain, rstd], writes=[hT])
            for m in col_list:
                ps = mm()
                for c in range(8):
                    P.op("pe", lambda e, c=c, m=m, ps=ps: e.matmul(ps[:], lhsT=W[:, c, m * 128:(m + 1) * 128], rhs=hT[:, c, :], start=(c == 0), stop=(c == 7)),
                         reads=[W, hT], writes=[ps])
                st = stg()
                if evi % 2 == 0:
                    P.op("act", lambda e, ps=ps, st=st: e.activation(out=st[:], in_=ps[:], func=AF.Copy), reads=[ps], writes=[st])
                else:
                    P.op("dve", lambda e, ps=ps, st=st: e.tensor_copy(out=st[:], in_=ps[:]), reads=[ps], writes=[st])
                evi += 1
                P.dma("pool", PT[m, :, j * TS:(j + 1) * TS], st[:], reads=[st], writes=[PT])
            for s in (range(TS // 128) if do_v else []):
                ps = vps()
                for c in range(8):
                    P.op("pe", lambda e, c=c, s=s, ps=ps: e.matmul(ps[:], lhsT=hT[:, c, s * 128:(s + 1) * 128], rhs=W[:, c, INW - 384:INW], start=(c == 0), stop=(c == 7)),
                         reads=[W, hT], writes=[ps])
                st = vst()
                P.op("act", lambda e, ps=ps, st=st: e.activation(out=st[:], in_=ps[:], func=AF.Copy), reads=[ps], writes=[st])
                r0 = j * TS + s * 128
                P.dma("pool", Vtok[r0:r0 + 128, :], st[:], reads=[st], writes=[Vtok])
        P.phase_barrier()
    P.es = es_outer


CK = 128
CDEC = 0.6065306597126334
NEG = -1.0e30


def phase_pool(P, es_outer, PT, poolw_d, pscale_d, invw_d, edge_d, YP, NTl):
    with ExitStack() as es:
        P.es = es
        Wn = NTl + 16
        pw = P.sb([128, 2, 128], BF16, "pw")
        psc = P.sb([128, 2], F32, "psc"); invw = P.sb([128, 2], F32, "invw"); edge = P.sb([128, 2, 16], F32, "edge")
        for c in range(2):
            P.dma("pool", pw[:, c, :], poolw_d[c, :, :], reads=[poolw_d], writes=[pw])
        P.dma("sp", psc[:], pscale_d[:], reads=[pscale_d], writes=[psc])
        P.dma("sp", invw[:], invw_d[:], reads=[invw_d], writes=[invw])
        P.dma("sp", edge[:], edge_d[:], reads=[edge_d], writes=[edge])
        U = P.sb([128, Wn], F32, "U"); S1 = P.sb([128, Wn], F32, "S1"); S2 = P.sb([128, Wn], F32, "S2")
        tmp = P.sb([128, NTl], F32, "ptmp")
        pb = P.sb([128, NTl], BF16, "pb")
        mm = rot([P.ps([128, TS], F32, f"pm{i}") for i in range(2)])
        stg = rot([P.sb([128, TS], F32, f"pst{i}") for i in range(2)])
        wins = (2, 4, 8, 16)
        for c in range(2):
            P.dma("sp", U[:], PT[c, :, HL - 8:HL + NTl + 8], reads=[PT], writes=[U])
            for half in range(2):
                W = wins[2 * c + half]
                pr = slice(64 * half, 64 * half + 64)
                src = U; cur = 1; bufs = [S1, S2]; bi = 0
                while cur < W:
                    dst = bufs[bi]; bi ^= 1
                    L = Wn - 2 * cur + 1
                    P.o("dve", "tensor_tensor", [src], [dst], out=dst[pr, 0:L], in0=src[pr, 0:L], in1=src[pr, cur:cur + L], op=ALU.add)
                    src = dst; cur *= 2
                o0 = 8 - W // 2
                P.o("dve", "tensor_scalar", [src, invw], [tmp], out=tmp[pr, :], in0=src[pr, o0:o0 + NTl], scalar1=invw[pr, c:c + 1], scalar2=0.0, op0=ALU.mult, op1=ALU.add)
                P.o("dve", "tensor_tensor", [tmp, edge], [tmp], out=tmp[pr, 0:8], in0=tmp[pr, 0:8], in1=edge[pr, c, 0:8], op=ALU.mult)
                P.o("dve", "tensor_tensor", [tmp, edge], [tmp], out=tmp[pr, NTl - 8:NTl], in0=tmp[pr, NTl - 8:NTl], in1=edge[pr, c, 8:16], op=ALU.mult)
                P.o("dve", "tensor_tensor", [tmp, U], [pb], out=pb[pr, :], in0=tmp[pr, :], in1=U[pr, 8:8 + NTl], op=ALU.subtract)
            for j in range(NTl // TS):
                ps = mm()
                P.o("pe", "matmul", [pw, pb], [ps], ps[:], lhsT=pw[:, c, :], rhs=pb[:, j * TS:(j + 1) * TS], start=True, stop=True)
                st = stg()
                P.o("dve", "tensor_scalar", [ps, psc], [st], out=st[:], in0=ps[:], scalar1=psc[:, c:c + 1], scalar2=0.0, op0=ALU.mult, op1=ALU.add)
                P.dma("sp", YP[c, :, j * TS:(j + 1) * TS], st[:], reads=[st], writes=[YP])
        P.phase_barrier()
    P.es = es_outer


def phase_att(P, es_outer, PT, Vtok, bias_d, kmask_d, YA, NTl):
    SB = 2048
    with ExitStack() as es:
        P.es = es
        bias = P.sb([128, 36, 128], F32, "abias")
        P.dma("sp", bias[:], bias_d[:], reads=[bias_d], writes=[bias])
        qT = [P.sb([64, SB], BF16, f"qT{h}") for h in range(6)]
        kT = [P.sb([64, SB + 2 * HL], BF16, f"kT{h}") for h in range(6)]
        acc = [P.sb([128, SB], F32, f"acc{h}") for h in range(6)]
        vaug = rot([P.sb([128, 6, 128], BF16, f"vaug{i}") for i in range(4)])
        for i in range(4):
            v = vaug()
            P.o("pool", "memset", [], [v], v[:], 1.0)
        nkt = (NTl // SB) * 3 * 16 * 2
        kmall = P.sb([128, nkt], F32, "kmall")
        P.dma("sp", kmall[:], kmask_d[:], reads=[kmask_d], writes=[kmall])
        sps = rot([P.ps([128, 512], F32, f"sps{i}") for i in range(4)])
        opsr = rot([P.ps([128, 512], F32, f"ops{i}") for i in range(3)])
        lg = rot([P.sb([128, 128], F32, f"lg{i}") for i in range(4)])
        pT = rot([P.sb([128, 128], BF16, f"pT{i}") for i in range(4)])
        den = P.sb([64, SB], F32, "den")
        yst = P.sb([64, SB], F32, "yst")
        for sb in range(NTl // SB):
            t0 = sb * SB
            for h in range(6):
                pr = slice(64 * (h % 2), 64 * (h % 2) + 64)
                P.dma("pool", qT[h][:], PT[13 + h // 2, pr, HL + t0:HL + t0 + SB], reads=[PT], writes=[qT[h]])
                P.dma("pool", kT[h][:], PT[16 + h // 2, pr, t0:t0 + SB + 2 * HL], reads=[PT], writes=[kT[h]])
            for br, dl in enumerate((1, 4, 16)):
                nb = SB // (128 * dl)
                for r in range(dl):
                    for b in range(nb):
                        qs = r + dl * 128 * b
                        vas = []; kss = []; kids = []
                        for kt in range(2):
                            ks = HL + r + dl * (128 * b - 64 + 128 * kt)
                            va = vaug()
                            kid = ((sb * 3 + br) * 16 + (r * nb + b)) * 2 + kt
                            e0 = t0 + ks
                            P.dma("sp", va[:, :, 0:64], Vtok[e0:e0 + 127 * dl + 1:dl, :].rearrange("p (h d) -> p h d", h=6), reads=[Vtok], writes=[va])
                            vas.append(va); kss.append(ks); kids.append(kid)
                        def s_pair(h):
                            outp = []
                            for kt in range(2):
                                ks = kss[kt]
                                sp_ = sps()
                                P.o("pe", "matmul", [kT[h], qT[h]], [sp_], sp_[:, 0:128], lhsT=kT[h][:, ks:ks + 127 * dl + 1:dl], rhs=qT[h][:, qs:qs + 127 * dl + 1:dl], start=True, stop=True)
                                l_ = lg()
                                P.o("dve", "scalar_tensor_tensor", [sp_, bias], [l_], out=l_[:], in0=sp_[:, 0:128], scalar=0.125, in1=bias[:, (br * 6 + h) * 2 + kt, :], op0=ALU.mult, op1=ALU.add)
                                p_ = pT()
                                P.o("act", "activation", [l_, kmall], [p_], out=p_[:], in_=l_[:], func=AF.Exp, bias=kmall[:, kids[kt]:kids[kt] + 1], scale=1.0)
                                outp.append(p_)
                            return outp
                        pend = s_pair(0)
                        for h in range(6):
                            nxt = s_pair(h + 1) if h + 1 < 6 else None
                            op_ = opsr()
                            for kt in range(2):
                                P.o("pe", "matmul", [vas[kt], pend[kt]], [op_], op_[:, 0:128], lhsT=vas[kt][:, h, :], rhs=pend[kt][:], start=(kt == 0), stop=(kt == 1))
                            pend = nxt
                            dst = acc[h][:, qs:qs + 127 * dl + 1:dl]
                            if br == 0:
                                if h % 2:
                                    P.o("dve", "tensor_copy", [op_], [acc[h]], out=dst, in_=op_[:, 0:128])
                                else:
                                    P.o("act", "copy", [op_], [acc[h]], out=dst, in_=op_[:, 0:128])
                            else:
                                P.o("dve", "tensor_tensor", [op_, acc[h]], [acc[h]], out=dst, in0=op_[:, 0:128], in1=dst, op=ALU.add)
            for h in range(6):
                P.dma("sp", den[:], acc[h][64:128, :], reads=[acc[h]], writes=[den])
                P.o("dve", "reciprocal", [den], [den], out=den[:], in_=den[:])
                P.o("dve", "tensor_tensor", [acc[h], den], [yst], out=yst[:], in0=acc[h][0:64, :], in1=den[:], op=ALU.mult)
                P.dma("sp", YA[h, :, t0:t0 + SB], yst[:], reads=[yst], writes=[YA])
        P.phase_barrier()
    P.es = es_outer


TL = 512


def make_ident(P, identf, identb):
    P.o("pool", "memset", [], [identf], identf[:], 1.0)
    P.o("pool", "affine_select", [identf], [identf], out=identf[:], in_=identf[:], pattern=[[-1, 128]], compare_op=ALU.is_equal, fill=0.0, base=0, channel_multiplier=1)
    P.o("pool", "tensor_copy", [identf], [identb], out=identb[:], in_=identf[:])


def phase_rwkv(P, es_outer, PT, convw_d, hp_d, dirp_d, dup_d, iup_d, gup_d, maskA_d, maskT_d,
               YS, QP, BON, GT, HEND, NTl):
    nt = NTl // TL
    with ExitStack() as es:
        P.es = es
        identf = P.sb([128, 128], F32, "identf"); identb = P.sb([128, 128], BF16, "identb")
        make_ident(P, identf, identb)
        ones64 = P.sb([64, 64], F32, "ones64")
        P.o("pool", "memset", [], [ones64], ones64[:], 1.0)
        mask0 = P.sb([64, TL], F32, "mask0")
        P.o("pool", "memset", [], [mask0], mask0[:], 1.0)
        P.o("pool", "memset", [mask0], [mask0], mask0[:, 0:TL:128], 0.0)
        convw = P.sb([64, 6, 3, 3], F32, "convw"); hp = P.sb([64, 6, 5], F32, "hp"); dirp = P.sb([64, 6, 2, 2], F32, "dirp")
        omka = P.sb([64, 6], F32, "omka")
        P.dma("sp", convw[:], convw_d[:], reads=[convw_d], writes=[convw])
        P.dma("sp", hp[:], hp_d[:], reads=[hp_d], writes=[hp])
        P.dma("sp", dirp[:], dirp_d[:], reads=[dirp_d], writes=[dirp])
        P.o("dve", "tensor_scalar", [hp], [omka], out=omka[:], in0=hp[:, :, 1], scalar1=-1.0, scalar2=1.0, op0=ALU.mult, op1=ALU.add)
        dup = P.sb([64, 2, 384], BF16, "dup"); iup = P.sb([64, 2, 384], BF16, "iup"); gup = P.sb([128, 384], BF16, "gup")
        for d in range(2):
            P.dma("pool", dup[:, d, :], dup_d[d, :, :], reads=[dup_d], writes=[dup])
            P.dma("pool", iup[:, d, :], iup_d[d, :, :], reads=[iup_d], writes=[iup])
        P.dma("pool", gup[:], gup_d[:], reads=[gup_d], writes=[gup])
        maskA = P.sb([128, 2, 128], F32, "maskA"); maskT = P.sb([128, 2, 640], F32, "maskT")
        P.dma("sp", maskA[:], maskA_d[:], reads=[maskA_d], writes=[maskA])
        P.dma("sp", maskT[:], maskT_d[:], reads=[maskT_d], writes=[maskT])
        tanhw = P.sb([64, NTl], BF16, "tanhw"); alat = P.sb([64, NTl], BF16, "alat"); sgl = P.sb([128, NTl], BF16, "sgl")
        latp = rot([P.sb([64, TL], F32, f"lat{i}") for i in range(1)]); glatp = rot([P.sb([128, TL], F32, f"glat{i}") for i in range(1)])
        P.dma("pool", alat[:], PT[11, 64:128, HL:HL + NTl], reads=[PT], writes=[alat])
        for j in range(nt):
            lat = latp(); glat = glatp()
            P.dma("sp", lat[:], PT[11, 0:64, HL + j * TL:HL + (j + 1) * TL], reads=[PT], writes=[lat])
            P.o("act", "activation", [lat], [tanhw], out=tanhw[:, j * TL:(j + 1) * TL], in_=lat[:], func=AF.Tanh)
            P.dma("sp", glat[:], PT[12, :, HL + j * TL:HL + (j + 1) * TL], reads=[PT], writes=[glat])
            P.o("act", "activation", [glat], [sgl], out=sgl[:, j * TL:(j + 1) * TL], in_=glat[:], func=AF.Sigmoid)
        P.nbuf += 1
        RKV = P.dram(f"RKVs{P.nbuf}", [4, 64, NTl], F32)
        rawp = [rot([P.sb([64, TL + 2], F32, f"raw{w}_{i}") for i in range(1)]) for w in range(3)]
        vbfp = rot([P.sb([64, TL], BF16, f"vbft{i}") for i in range(2)])
        YSa = P.sb([64, NTl], F32, "YSa"); BONa = P.sb([64, NTl], F32, "BONa")
        pg = rot([P.ps([128, 512], F32, f"pg{i}") for i in range(6)])
        pHd = [P.ps([128, 512], F32, f"pH{d}") for d in range(2)]
        def t64(name, dt=F32, w=TL, n=2):
            return rot([P.sb([64, w], dt, f"{name}{i}") for i in range(n)])
        def t128(name, w, dt=BF16, n=2):
            return rot([P.sb([128, w], dt, f"{name}{i}") for i in range(n)])
        tA = t64("tA"); tB = t64("tB"); tC = t64("tC"); tD = t64("tD"); tE = t64("tE"); tF = t64("tF"); tG = t64("tG", n=1); tH = t64("tH", n=1)
        tI = t64("tI", n=1); tJ = t64("tJ"); tK = t64("tK"); tL_ = t64("tL")
        gst = t64("gst")
        rrp = t64("rrt", n=1); kcp = t64("kct", n=1); kkp = t64("kkt", n=1); vvp = t64("vvt", n=1)
        bA = t64("bA", BF16); bR = t64("bR", BF16); bB = t64("bB", BF16); bK = t64("bK", BF16); bBh = t64("bBh", BF16); bKh = t64("bKh", BF16)
        pcs = rot([P.sb([64, 4], F32, f"pc{i}") for i in range(2)])
        ARp = rot([P.sb([64, 4, 256], BF16, f"AR{i}") for i in range(2)])
        Xp = t128("Xp", 128, BF16, n=8); XXp = t128("XXp", 256, BF16, n=16); LTRBp = t128("LTRB", 256, n=8); AKRKp = t128("AKRK", 256, n=8); TMp = t128("TM", 256, n=8)
        LTfp = t128("LTf", 128, BF16, n=8); L21p = t128("L21", 128, BF16, n=8); Z1p = t128("Z1", 128, BF16, n=8); Z2p = t128("Z2", 128, BF16, n=8)
        TTp = t128("TT", 128, BF16, n=16); AKVp = t128("AKV", 64, n=8); WUp = t128("WU", 128, n=8)
        QeTp = t64("QeT", BF16, 128, n=16); MTp = t64("MT", F32, 64, n=16); Gp = t64("G", F32, 64, n=16); QPs = t128("QPs", 128, BF16, n=4)
        HPpd = [rot([P.sb([64, 128], F32, f"HP{d}_{i}") for i in range(2)]) for d in range(2)]
        HPbpd = [rot([P.sb([64, 128], BF16, f"HPb{d}_{i}") for i in range(2)]) for d in range(2)]

        for h in range(6):
            pr = slice(64 * (h % 2), 64 * (h % 2) + 64)
            hc = slice(h * 64, (h + 1) * 64)
            for j in range(nt):
                ts_ = slice(j * TL, (j + 1) * TL)
                o3 = (tI(), tJ(), tK())
                for w3 in range(3):
                    rw = rawp[w3]()
                    P.dma("sp", rw[:], PT[2 + 3 * w3 + h // 2, pr, HL - 1 + j * TL:HL + (j + 1) * TL + 1], reads=[PT], writes=[rw])
                    o_ = o3[w3]
                    P.o("dve", "tensor_scalar", [rw, convw], [o_], out=o_[:], in0=rw[:, 0:TL], scalar1=convw[:, h, w3, 0:1], scalar2=0.0, op0=ALU.mult, op1=ALU.add)
                    P.o("dve", "scalar_tensor_tensor", [rw, convw, o_], [o_], out=o_[:], in0=rw[:, 1:TL + 1], scalar=convw[:, h, w3, 1:2], in1=o_[:], op0=ALU.mult, op1=ALU.add)
                    P.o("dve", "scalar_tensor_tensor", [rw, convw, o_], [o_], out=o_[:], in0=rw[:, 2:TL + 2], scalar=convw[:, h, w3, 2:3], in1=o_[:], op0=ALU.mult, op1=ALU.add)
                r_t, kc_t, v_t = o3
                kk_t = tL_()
                P.o("dve", "tensor_scalar", [kc_t, hp], [kk_t], out=kk_t[:], in0=kc_t[:], scalar1=hp[:, h, 0:1], scalar2=0.0, op0=ALU.mult, op1=ALU.add)
                sq = tA()
                P.o("pool", "tensor_tensor", [kk_t], [sq], out=sq[:], in0=kk_t[:], in1=kk_t[:], op=ALU.mult)
                p = pg()
                P.o("pe", "matmul", [ones64, sq], [p], p[0:64, :], lhsT=ones64[:], rhs=sq[:], start=True, stop=True)
                nr = tB()
                P.o("act", "activation", [p], [nr], out=nr[:], in_=p[0:64, :], func=AF.Sqrt)
                P.o("dve", "tensor_scalar", [nr], [nr], out=nr[:], in0=nr[:], scalar1=1e-12, scalar2=0.0, op0=ALU.max, op1=ALU.add)
                P.o("dve", "reciprocal", [nr], [nr], out=nr[:], in_=nr[:])
                P.o("dve", "tensor_tensor", [kk_t, nr], [kk_t], out=kk_t[:], in0=kk_t[:], in1=nr[:], op=ALU.mult)
                for qi, tt_ in enumerate((r_t, kc_t, kk_t, v_t)):
                    P.dma("sp", RKV[qi, :, ts_], tt_[:], reads=[tt_], writes=[RKV])
                p = pg()
                P.o("pe", "matmul", [gup, sgl], [p], p[0:64, :], lhsT=gup[:, hc], rhs=sgl[:, ts_], start=True, stop=True)
                g_ = gst()
                P.o("act", "copy", [p], [g_], out=g_[:], in_=p[0:64, :])
                P.dma("sp", GT[h, :, ts_], g_[:], reads=[g_], writes=[GT])
            P.o("pool", "memset", [], [YSa], YSa[:], 0.0)
            P.o("pool", "memset", [], [BONa], BONa[:], 0.0)
            HPd = {}; HPbd = {}
            for d in range(2):
                HP = HPpd[d](); HPb = HPbpd[d]()
                P.o("pool", "memset", [], [HP], HP[:, 0:64], 0.0)
                P.o("pool", "tensor_copy", [identf], [HP], out=HP[:, 64:128], in_=identf[0:64, 0:64])
                P.o("pool", "tensor_copy", [HP], [HPb], out=HPb[:], in_=HP[:])
                HPd[d] = HP; HPbd[d] = HPb
            def tile_work(d, j):
                if True:
                    ts_ = slice(j * TL, (j + 1) * TL)
                    rr = rrp(); kc = kcp(); kk = kkp(); vv = vvp()
                    for qi, tt_ in enumerate((rr, kc, kk, vv)):
                        P.dma("sp", tt_[:], RKV[qi, :, ts_], reads=[RKV], writes=[tt_])
                    vbf = vbfp()
                    P.o("pool", "tensor_copy", [vv], [vbf], out=vbf[:], in_=vv[:])
                    p = pg()
                    P.o("pe", "matmul", [dup, tanhw], [p], p[0:64, :], lhsT=dup[:, d, hc], rhs=tanhw[:, ts_], start=True, stop=True)
                    sg = tA()
                    P.o("act", "activation", [p, dirp], [sg], out=sg[:], in_=p[0:64, :], func=AF.Sigmoid, bias=dirp[:, h, d, 0:1], scale=1.0)
                    p = pg()
                    P.o("pe", "matmul", [iup, alat], [p], p[0:64, :], lhsT=iup[:, d, hc], rhs=alat[:, ts_], start=True, stop=True)
                    aa = tB()
                    P.o("act", "activation", [p, dirp], [aa], out=aa[:], in_=p[0:64, :], func=AF.Sigmoid, bias=dirp[:, h, d, 1:2], scale=1.0)
                    t1 = tC()
                    P.o("dve", "tensor_scalar", [aa, hp, omka], [t1], out=t1[:], in0=aa[:], scalar1=hp[:, h, 1:2], scalar2=omka[:, h:h + 1], op0=ALU.mult, op1=ALU.add)
                    keff = tD()
                    P.o("dve", "tensor_tensor", [kc, t1], [keff], out=keff[:], in0=kc[:], in1=t1[:], op=ALU.mult)
                    bb = tE()
                    P.o("pool", "tensor_tensor", [kk, aa], [bb], out=bb[:], in0=kk[:], in1=aa[:], op=ALU.mult)
                    tb = tC()
                    P.o("dve", "scalar_tensor_tensor", [rr, hp, keff], [tb], out=tb[:], in0=rr[:], scalar=hp[:, h, 2:3], in1=keff[:], op0=ALU.mult, op1=ALU.mult)
                    p = pg()
                    P.o("pe", "matmul", [ones64, tb], [p], p[0:64, :], lhsT=ones64[:], rhs=tb[:], start=True, stop=True)
                    t2 = tF()
                    P.o("dve", "tensor_tensor", [p, vv], [t2], out=t2[:], in0=p[0:64, :], in1=vv[:], op=ALU.mult)
                    P.o("pool", "tensor_tensor", [t2, BONa], [BONa], out=BONa[:, ts_], in0=t2[:], in1=BONa[:, ts_], op=ALU.add)
                    cs = tF()
                    P.o("dve", "tensor_tensor_scan", [mask0, sg], [cs], out=cs[:], data0=mask0[:], data1=sg[:], initial=0.0, op0=ALU.mult, op1=ALU.add)
                    csx = tG()
                    P.o("pool", "tensor_tensor", [cs, sg], [csx], out=csx[:], in0=cs[:], in1=sg[:], op=ALU.subtract)
                    rem = tH()
                    cs3 = cs[:].rearrange("p (c t) -> p c t", t=128)
                    P.o("dve", "tensor_tensor", [cs], [rem], out=rem[:].rearrange("p (c t) -> p c t", t=128), in0=cs3[:, :, 127:128].to_broadcast([64, 4, 128]), in1=cs3, op=ALU.subtract)
                    pc = pcs()
                    P.o("act", "activation", [cs], [pc], out=pc[:], in_=cs3[:, :, 127], func=AF.Exp, scale=-CDEC)
                    if d == 0:
                        Ein, Eex, Erem = cs, csx, rem
                    else:
                        remx = tI()
                        P.o("pool", "tensor_tensor", [rem, sg], [remx], out=remx[:], in0=rem[:], in1=sg[:], op=ALU.add)
                        Ein, Eex, Erem = remx, rem, csx
                    Pin = tJ(); Pex = tK(); Pinv = tL_(); Prem = tC()
                    P.o("act", "activation", [Ein], [Pin], out=Pin[:], in_=Ein[:], func=AF.Exp, scale=-CDEC)
                    P.o("act", "activation", [Eex], [Pex], out=Pex[:], in_=Eex[:], func=AF.Exp, scale=-CDEC)
                    P.o("act", "activation", [Ein], [Pinv], out=Pinv[:], in_=Ein[:], func=AF.Exp, scale=CDEC)
                    P.o("act", "activation", [Erem], [Prem], out=Prem[:], in_=Erem[:], func=AF.Exp, scale=-CDEC)
                    AR = ARp(); bT = bB(); kT = bK(); bh = bBh(); kh = bKh()
                    v3 = lambda ap: ap.rearrange("p (c t) -> p c t", t=128)
                    P.o("dve", "scalar_tensor_tensor", [kk, Pex], [AR], out=AR[:, :, 0:128], in0=v3(kk[:]), scalar=-1.0, in1=v3(Pex[:]), op0=ALU.mult, op1=ALU.mult)
                    P.o("pool", "tensor_tensor", [rr, Pin], [AR], out=AR[:, :, 128:256], in0=v3(rr[:]), in1=v3(Pin[:]), op=ALU.mult)
                    P.o("dve", "tensor_tensor", [bb, Pinv], [bT], out=bT[:], in0=bb[:], in1=Pinv[:], op=ALU.mult)
                    P.o("pool", "tensor_tensor", [keff, Pinv], [kT], out=kT[:], in0=keff[:], in1=Pinv[:], op=ALU.mult)
                    P.o("dve", "tensor_tensor", [bb, Prem], [bh], out=bh[:], in0=bb[:], in1=Prem[:], op=ALU.mult)
                    P.o("pool", "tensor_tensor", [keff, Prem], [kh], out=kh[:], in0=keff[:], in1=Prem[:], op=ALU.mult)
                    chunks = list(range(4)) if d == 0 else list(range(3, -1, -1))
                    def local_gen(c, res):
                        sl = slice(c * 128, (c + 1) * 128)
                        tok = slice(j * TL + c * 128, j * TL + (c + 1) * 128)
                        yield
                        p = pg()
                        P.o("pe", "matmul", [AR, bT], [p], p[:, 0:128], lhsT=AR[:, c, 0:128], rhs=bT[:, sl], start=True, stop=True)
                        X = Xp()
                        P.o("dve", "tensor_tensor", [p, maskT], [X], out=X[:], in0=p[:, 0:128], in1=maskT[:, d, 256:384], op=ALU.mult)
                        yield
                        p = pg()
                        P.o("pe", "matmul", [bT, AR], [p], p[:, 0:256], lhsT=bT[:, sl], rhs=AR[:, c, :], start=True, stop=True)
                        LTRB = LTRBp()
                        P.o("dve", "tensor_tensor", [p, maskT], [LTRB], out=LTRB[:], in0=p[:, 0:256], in1=maskT[:, d, 0:256], op=ALU.mult)
                        LTf = LTfp()
                        P.o("dve", "tensor_tensor", [p, maskT], [LTf], out=LTf[:], in0=p[:, 0:128], in1=maskT[:, d, 384:512], op=ALU.mult)
                        L21T = L21p()
                        P.o("dve", "tensor_tensor", [p, maskT], [L21T], out=L21T[:], in0=p[:, 0:128], in1=maskT[:, d, 512:640], op=ALU.mult)
                        yield
                        p = pg()
                        P.o("pe", "matmul", [kT, AR], [p], p[:, 0:256], lhsT=kT[:, sl], rhs=AR[:, c, :], start=True, stop=True)
                        AKRK = AKRKp()
                        P.o("dve", "tensor_tensor", [p, maskT], [AKRK], out=AKRK[:], in0=p[:, 0:256], in1=maskT[:, d, 0:256], op=ALU.mult)
                        yield
                        p = pg()
                        P.o("pe", "matmul", [AR, identb], [p], p[:, 0:64], lhsT=AR[:, c, 0:128], rhs=identb[0:64, 0:64], start=True, stop=True)
                        for qi, src in ((1, bh), (2, kh)):
                            P.o("pe", "matmul", [src, identb], [p], p[:, qi * 64:(qi + 1) * 64], lhsT=src[:, sl], rhs=identb[0:64, 0:64], start=True, stop=True)
                        P.o("pe", "matmul", [vbf, identb], [p], p[:, 192:256], lhsT=vbf[:, sl], rhs=identb[0:64, 0:64], start=True, stop=True)
                        TM = TMp()
                        P.o("act", "copy", [p], [TM], out=TM[:], in_=p[:, 0:256])
                        yield
                        TT = TTp()
                        P.o("pool", "tensor_tensor", [identb, LTf], [TT], out=TT[:], in0=identb[:], in1=LTf[:], op=ALU.add)
                        Xc = X[:]; XTc = LTf[:]; Xb = X; XTb = LTf
                        for i in range(5):
                            p = pg()
                            P.o("pe", "matmul", [Xb, XTb], [p], p[:, 0:128], lhsT=XTc, rhs=Xc, start=True, stop=True)
                            if i < 4:
                                P.o("pe", "matmul", [Xb, XTb], [p], p[:, 128:256], lhsT=Xc, rhs=XTc, start=True, stop=True)
                            XX = XXp()
                            if i < 4:
                                P.o("act", "copy", [p], [XX], out=XX[:], in_=p[:, 0:256])
                            else:
                                P.o("act", "copy", [p], [XX], out=XX[:, 0:128], in_=p[:, 0:128])
                            Xc = XX[:, 0:128]; XTc = XX[:, 128:256]; Xb = XX; XTb = XX
                            yield
                            p2 = pg()
                            P.o("pe", "matmul", [XX, TT], [p2], p2[:, 0:128], lhsT=Xc, rhs=TT[:], start=True, stop=True)
                            TTn = TTp()
                            P.o("dve", "tensor_tensor", [p2, TT], [TTn], out=TTn[:], in0=p2[:, 0:128], in1=TT[:], op=ALU.add)
                            TT = TTn
                            yield
                        yield
                        p = pg()
                        P.o("pe", "matmul", [AKRK, TM], [p], p[:, 0:64], lhsT=AKRK[:, 0:128], rhs=TM[:, 192:256], start=True, stop=True)
                        AKV = AKVp()
                        P.o("act", "copy", [p], [AKV], out=AKV[:], in_=p[:, 0:64])
                        yield
                        p = pg()
                        P.o("pe", "matmul", [TT, TM], [p], p[:, 0:64], lhsT=TT[:], rhs=TM[:, 0:64], start=True, stop=True)
                        P.o("pe", "matmul", [TT, AKV], [p], p[:, 64:128], lhsT=TT[:], rhs=AKV[:], start=True, stop=True)
                        Z1 = Z1p()
                        P.o("act", "copy", [p], [Z1], out=Z1[:], in_=p[:, 0:128])
                        yield
                        p = pg()
                        P.o("pe", "matmul", [L21T, Z1], [p], p[:, 0:128], lhsT=L21T[:], rhs=Z1[:], start=True, stop=True)
                        Z2 = Z2p()
                        P.o("act", "copy", [p], [Z2], out=Z2[:], in_=p[:, 0:128])
                        yield
                        p = pg()
                        P.o("pe", "matmul", [TT, Z2], [p], p[:, 0:128], lhsT=TT[:], rhs=Z2[:], start=True, stop=True)
                        WU = WUp()
                        P.o("dve", "tensor_tensor", [p, Z1], [WU], out=WU[:], in0=p[:, 0:128], in1=Z1[:], op=ALU.add)
                        yield
                        p = pg()
                        P.o("pe", "matmul", [WU, LTRB], [p], p[0:64, 0:128], lhsT=WU[:, 0:64], rhs=LTRB[:, 128:256], start=True, stop=True)
                        QeT = QeTp()
                        P.o("dve", "tensor_tensor", [p, AR], [QeT], out=QeT[:], in0=p[0:64, 0:128], in1=AR[:, c, 128:256], op=ALU.add)
                        yield
                        p = pg()
                        P.o("pe", "matmul", [WU, LTRB], [p], p[0:64, 0:128], lhsT=WU[:, 64:128], rhs=LTRB[:, 128:256], start=True, stop=False)
                        P.o("pe", "matmul", [TM, AKRK], [p], p[0:64, 0:128], lhsT=TM[:, 192:256], rhs=AKRK[:, 128:256], start=False, stop=True)
                        P.o("dve", "tensor_tensor", [p, YSa], [YSa], out=YSa[:, tok], in0=p[0:64, 0:128], in1=YSa[:, tok], op=ALU.add)
                        yield
                        p = pg()
                        P.o("pe", "matmul", [WU, TM], [p], p[0:64, 0:64], lhsT=WU[:, 0:64], rhs=TM[:, 64:128], start=True, stop=True)
                        MT = MTp()
                        P.o("dve", "scalar_tensor_tensor", [identf, pc, p], [MT], out=MT[:], in0=identf[0:64, 0:64], scalar=pc[:, c:c + 1], in1=p[0:64, 0:64], op0=ALU.mult, op1=ALU.add)
                        yield
                        p = pg()
                        P.o("pe", "matmul", [TM, WU], [p], p[0:64, 0:64], lhsT=TM[:, 64:128], rhs=WU[:, 64:128], start=True, stop=False)
                        P.o("pe", "matmul", [TM], [p], p[0:64, 0:64], lhsT=TM[:, 128:192], rhs=TM[:, 192:256], start=False, stop=True)
                        G = Gp()
                        P.o("act", "copy", [p], [G], out=G[:], in_=p[0:64, 0:64])
                        res.update(QeT=QeT, MT=MT, G=G, tok=tok)
                        yield
                    results = {c: {} for c in chunks}
                    gens = [local_gen(c, results[c]) for c in chunks]
                    return chunks, results, gens
            def chain_gen(d, chunks, results):
                HP = HPd[d]; HPb = HPbd[d]
                for c in chunks:
                    QeT = results[c]['QeT']; MT = results[c]['MT']; G = results[c]['G']; tok = results[c]['tok']
                    p = pg()
                    P.o("pe", "matmul", [HPb, QeT], [p], p[:, 0:128], lhsT=HPb[:], rhs=QeT[:], start=True, stop=True)
                    P.o("dve", "tensor_tensor", [p, YSa], [YSa], out=YSa[:, tok], in0=p[0:64, 0:128], in1=YSa[:, tok], op=ALU.add)
                    qps = QPs()
                    P.o("act", "copy", [p], [qps], out=qps[64:128, :], in_=p[64:128, 0:128])
                    P.dma("sp", QP[d, h, :, tok], qps[64:128, :], reads=[qps], writes=[QP])
                    yield
                    pH_ = pHd[d]
                    P.o("pe", "matmul", [MT, HP], [pH_], pH_[0:64, 0:128], lhsT=MT[:], rhs=HP[:], start=True, stop=True)
                    HPn = HPpd[d](); HPbn = HPbpd[d]()
                    P.o("dve", "tensor_tensor", [pH_, G], [HPn], out=HPn[:, 0:64], in0=pH_[0:64, 0:64], in1=G[:], op=ALU.add)
                    P.o("act", "copy", [pH_], [HPn], out=HPn[:, 64:128], in_=pH_[0:64, 64:128])
                    yield
                    P.o("pool", "tensor_copy", [HPn], [HPbn], out=HPbn[:], in_=HPn[:])
                    HP = HPn; HPb = HPbn
                    HPd[d] = HP; HPbd[d] = HPb
                    yield

            def run_lockstep(gl):
                alive = list(gl)
                while alive:
                    nxt_alive = []
                    for g_ in alive:
                        try:
                            next(g_)
                            nxt_alive.append(g_)
                        except StopIteration:
                            pass
                    alive = nxt_alive

            carry = []
            for i in range(nt):
                pend = []
                for d in range(2):
                    j = i if d == 0 else nt - 1 - i
                    chunks, results, gens = tile_work(d, j)
                    pend.append((d, chunks, results, gens))
                run_lockstep(carry + [g_ for (_, _, _, gl) in pend for g_ in gl])
                carry = [chain_gen(d, chunks, results) for (d, chunks, results, _) in pend]
            run_lockstep(carry)
            for d in range(2):
                P.dma("sp", HEND[d, h, :, :], HPd[d][:], reads=[HPd[d]], writes=[HEND])
            P.dma("sp", YS[h, :, :], YSa[:], reads=[YSa], writes=[YS])
            P.dma("sp", BON[h, :, :], BONa[:], reads=[BONa], writes=[BON])
        P.phase_barrier()
    P.es = es_outer


T2 = 256
GN_EPS = 64e-5


def phase_fin(P, es_outer, YS, QP, BON, GT, PRED, hp_d, YR, NTl):
    TL = 512
    with ExitStack() as es:
        P.es = es
        hp = P.sb([64, 6, 5], F32, "hp4")
        P.dma("sp", hp[:], hp_d[:], reads=[hp_d], writes=[hp])
        o64 = P.sb([64, 64], F32, "o64")
        P.o("pool", "memset", [], [o64], o64[:], 1.0 / 64)
        epsg = P.sb([64, 1], F32, "epsg")
        P.o("pool", "memset", [], [epsg], epsg[:], GN_EPS)
        pg = rot([P.ps([128, 512], F32, f"fg{i}") for i in range(6)])
        pred = rot([P.sb([64, 128], F32, f"pred{i}") for i in range(3)])
        Sb = rot([P.sb([64, 64], F32, f"Sf{i}") for i in range(3)])
        Hb = [P.sb([64, 64], BF16, f"Hb{d}") for d in range(2)]
        def t64(name, dt=F32, n=2):
            return rot([P.sb([64, TL], dt, f"{name}{i}") for i in range(n)])
        ys_t = t64("ys"); qp_t = [t64("qp0", BF16), t64("qp1", BF16)]; bon_t = t64("bon"); g_t = t64("g")
        y_t = t64("y"); yc_t = t64("yc"); sq_t = t64("sq"); rs_t = t64("rs"); out_t = t64("out")
        for h in range(6):
            for d in range(2):
                S = Sb()
                P.o("pool", "memset", [], [S], S[:], 0.0)
                for slot in range(3):
                    pd = pred()
                    P.dma("sp", pd[:], PRED[d, h, slot, :, :], reads=[PRED], writes=[pd])
                    p = pg()
                    P.o("pe", "matmul", [pd, S], [p], p[0:64, 0:64], lhsT=pd[:, 0:64], rhs=S[:], start=True, stop=True)
                    Sn = Sb()
                    P.o("dve", "tensor_tensor", [p, pd], [Sn], out=Sn[:], in0=p[0:64, 0:64], in1=pd[:, 64:128], op=ALU.add)
                    S = Sn
                P.o("act", "copy", [S], [Hb[d]], out=Hb[d][:], in_=S[:])
            for j in range(NTl // TL):
                ts_ = slice(j * TL, (j + 1) * TL)
                ys = ys_t(); bon = bon_t(); g = g_t()
                P.dma("sp", ys[:], YS[h, :, ts_], reads=[YS], writes=[ys])
                P.dma("sp", bon[:], BON[h, :, ts_], reads=[BON], writes=[bon])
                P.dma("sp", g[:], GT[h, :, ts_], reads=[GT], writes=[g])
                p = pg()
                for d in range(2):
                    q = qp_t[d]()
                    P.dma("sp", q[:], QP[d, h, :, ts_], reads=[QP], writes=[q])
                    P.o("pe", "matmul", [Hb[d], q], [p], p[0:64, :], lhsT=Hb[d][:], rhs=q[:], start=(d == 0), stop=(d == 1))
                y = y_t()
                P.o("dve", "tensor_tensor", [p, ys], [y], out=y[:], in0=p[0:64, :], in1=ys[:], op=ALU.add)
                p = pg()
                P.o("pe", "matmul", [o64, y], [p], p[0:64, :], lhsT=o64[:], rhs=y[:], start=True, stop=True)
                yc = yc_t()
                P.o("dve", "tensor_tensor", [y, p], [yc], out=yc[:], in0=y[:], in1=p[0:64, :], op=ALU.subtract)
                sq = sq_t()
                P.o("pool", "tensor_tensor", [yc], [sq], out=sq[:], in0=yc[:], in1=yc[:], op=ALU.mult)
                p = pg()
                P.o("pe", "matmul", [o64, sq], [p], p[0:64, :], lhsT=o64[:], rhs=sq[:], start=True, stop=True)
                rs = rs_t()
                P.o("act", "activation", [p, epsg], [rs], out=rs[:], in_=p[0:64, :], func=AF.Sqrt, bias=epsg[:], scale=1.0)
                P.o("dve", "reciprocal", [rs], [rs], out=rs[:], in_=rs[:])
                P.o("dve", "tensor_tensor", [yc, rs], [yc], out=yc[:], in0=yc[:], in1=rs[:], op=ALU.mult)
                P.o("dve", "tensor_scalar", [yc, hp], [yc], out=yc[:], in0=yc[:], scalar1=hp[:, h, 3:4], scalar2=hp[:, h, 4:5], op0=ALU.mult, op1=ALU.add)
                P.o("pool", "tensor_tensor", [yc, bon], [yc], out=yc[:], in0=yc[:], in1=bon[:], op=ALU.add)
                o_ = out_t()
                P.o("dve", "tensor_tensor", [yc, g], [o_], out=o_[:], in0=yc[:], in1=g[:], op=ALU.mult)
                P.dma("sp", YR[h, :, ts_], o_[:], reads=[o_], writes=[YR])
        P.phase_barrier()
    P.es = es_outer


def phase_E(P, es_outer, xT, YP, YR, YA, wout_d, gpost_d, gffn_d, gpost2_d, wfi_d, wfo_d, XO, NTl, xoff):
    nt = NTl // T2
    with ExitStack() as es:
        P.es = es
        Wo = P.sb([128, 8, 1024], BF16, "Wo")
        Wi = P.sb([128, 8, 4096], BF16, "Wi"); Wo2 = P.sb([128, 32, 1024], BF16, "Wo2")
        for c in range(8):
            P.dma("pool", Wo[:, c, :], wout_d[c * 128:(c + 1) * 128, :], reads=[wout_d], writes=[Wo])
        for c in range(8):
            P.dma("pool", Wi[:, c, :], wfi_d[c, :, :], reads=[wfi_d], writes=[Wi])
        for c in range(32):
            P.dma("pool", Wo2[:, c, :], wfo_d[c, :, :], reads=[wfo_d], writes=[Wo2])
        g1 = P.sb([128, 8], F32, "g1"); g2 = P.sb([128, 8], F32, "g2"); g3 = P.sb([128, 8], F32, "g3")
        P.dma("sp", g1[:], gpost_d[:], reads=[gpost_d], writes=[g1])
        P.dma("sp", g2[:], gffn_d[:], reads=[gffn_d], writes=[g2])
        P.dma("sp", g3[:], gpost2_d[:], reads=[gpost2_d], writes=[g3])
        ones_bf = P.sb([128, 128], BF16, "ones4"); eps_t = P.sb([128, 1], F32, "eps4")
        P.o("dve", "memset", [], [ones_bf], ones_bf[:], 1.0)
        P.o("dve", "memset", [], [eps_t], eps_t[:], EPS)
        xs2 = [P.sb([128, 8, T2], F32, f"x4_{i}") for i in range(2)]
        ym2 = [P.sb([128, 8, T2], BF16, f"ym_{i}") for i in range(2)]
        mx = P.sb([128, 8, T2], F32, "mx")
        sq = P.sb([128, 8, T2], BF16, "sq4"); hT = P.sb([128, 8, T2], BF16, "hT4")
        aT = P.sb([128, 32, T2], BF16, "aT")
        tmp = P.sb([128, T2], F32, "tmp4"); rstd = P.sb([128, T2], F32, "rstd4")
        rl = rot([P.sb([128, T2], F32, f"rl{i}") for i in range(3)])
        ss_ps = P.ps([128, 512], F32, "ss4")
        mm = rot([P.ps([128, 512], F32, f"m4_{i}") for i in range(5)])
        def load(j):
            ts_ = slice(j * T2, (j + 1) * T2)
            b = j % 2
            P.dma("sp", xs2[b][:], xT[:, :, xoff + j * T2:xoff + (j + 1) * T2].rearrange("c p t -> p c t"), reads=[xT], writes=[xs2[b]])
            P.dma("pool", ym2[b][:, 0:2, :], YP[:, :, ts_].rearrange("c p t -> p c t"), reads=[YP], writes=[ym2[b]])
            P.dma("pool", ym2[b][:, 2:5, :], YR[:, :, ts_].rearrange("(c h) p t -> (h p) c t", h=2), reads=[YR], writes=[ym2[b]])
            P.dma("pool", ym2[b][:, 5:8, :], YA[:, :, ts_].rearrange("(c h) p t -> (h p) c t", h=2), reads=[YA], writes=[ym2[b]])
        def rms(src):
            P.o("act", "activation", [src], [sq], out=sq[:], in_=src[:], func=AF.Square)
            for c in range(8):
                P.o("pe", "matmul", [ones_bf, sq], [ss_ps], ss_ps[:, 0:T2], lhsT=ones_bf[:], rhs=sq[:, c, :], start=(c == 0), stop=(c == 7))
            P.o("act", "activation", [ss_ps, eps_t], [tmp], out=tmp[:], in_=ss_ps[:, 0:T2], func=AF.Sqrt, scale=1.0 / D, bias=eps_t[:])
            P.o("dve", "reciprocal", [tmp], [rstd], out=rstd[:], in_=tmp[:])
        load(0)
        ev = 0
        for j in range(nt):
            if j + 1 < nt:
                load(j + 1)
            b = j % 2
            xs = xs2[b]; ym = ym2[b]
            for m in range(8):
                ps = mm(); ms = slice(m * 128, (m + 1) * 128)
                for c in range(8):
                    P.o("pe", "matmul", [Wo, ym], [ps], ps[:, 0:T2], lhsT=Wo[:, c, ms], rhs=ym[:, c, :], start=(c == 0), stop=(c == 7))
                if m % 2:
                    P.o("act", "copy", [ps], [mx], out=mx[:, m, :], in_=ps[:, 0:T2])
                else:
                    P.o("dve", "tensor_copy", [ps], [mx], out=mx[:, m, :], in_=ps[:, 0:T2])
            rms(mx)
            for c in range(8):
                P.o("dve", "scalar_tensor_tensor", [mx, g1, rstd], [mx], out=mx[:, c, :], in0=mx[:, c, :], scalar=g1[:, c:c + 1], in1=rstd[:], op0=ALU.mult, op1=ALU.mult)
            P.o("pool", "tensor_tensor", [xs, mx], [xs], out=xs[:], in0=xs[:], in1=mx[:], op=ALU.add)
            rms(xs)
            for c in range(8):
                P.o("dve", "scalar_tensor_tensor", [xs, g2, rstd], [hT], out=hT[:, c, :], in0=xs[:, c, :], scalar=g2[:, c:c + 1], in1=rstd[:], op0=ALU.mult, op1=ALU.mult)
            for f in range(32):
                ps = mm()
                for c in range(8):
                    P.o("pe", "matmul", [Wi, hT], [ps], ps[:, 0:T2], lhsT=Wi[:, c, f * 128:(f + 1) * 128], rhs=hT[:, c, :], start=(c == 0), stop=(c == 7))
                r_ = rl()
                P.o("act", "activation", [ps], [r_], out=r_[:], in_=ps[:, 0:T2], func=AF.Relu)
                P.o("dve" if f % 2 else "pool", "tensor_tensor", [r_], [aT], out=aT[:, f, :], in0=r_[:], in1=r_[:], op=ALU.mult)
            for m in range(8):
                ps = mm(); ms = slice(m * 128, (m + 1) * 128)
                for f in range(32):
                    P.o("pe", "matmul", [Wo2, aT], [ps], ps[:, 0:T2], lhsT=Wo2[:, f, ms], rhs=aT[:, f, :], start=(f == 0), stop=(f == 31))
                if m % 2:
                    P.o("act", "copy", [ps], [mx], out=mx[:, m, :], in_=ps[:, 0:T2])
                else:
                    P.o("dve", "tensor_copy", [ps], [mx], out=mx[:, m, :], in_=ps[:, 0:T2])
            rms(mx)
            for c in range(8):
                P.o("dve", "scalar_tensor_tensor", [mx, g3, rstd], [mx], out=mx[:, c, :], in0=mx[:, c, :], scalar=g3[:, c:c + 1], in1=rstd[:], op0=ALU.mult, op1=ALU.mult)
            P.o("pool", "tensor_tensor", [xs, mx], [xs], out=xs[:], in0=xs[:], in1=mx[:], op=ALU.add)
            P.dma("sp", XO[:, :, j * T2:(j + 1) * T2].rearrange("c p t -> p c t"), xs[:], reads=[xs], writes=[XO])
        P.phase_barrier()
    P.es = es_outer

NEGV = -1.0e30

def t5_bucket_np(rel):
    nb = 16; max_exact = 8
    ret = (rel > 0).astype(np.int32) * nb
    n = np.abs(rel)
    nf = np.maximum(n, max_exact).astype(np.float32)
    large = max_exact + (np.log(nf / np.float32(max_exact)) / np.float32(np.log(1024 / max_exact)) * np.float32(nb - max_exact)).astype(np.int32)
    large = np.minimum(large, nb - 1)
    return ret + np.where(n < max_exact, n, large)

def att_bias_tiles(rel_bias):
    out = np.full((128, 36, 128), NEGV, np.float32)
    j = np.arange(128)[:, None]; i = np.arange(128)[None, :]
    for br, dl in enumerate((1, 4, 16)):
        for kt in range(2):
            rel = (128 * kt + j - 64) - i
            valid = np.abs(rel) <= 64
            bk = t5_bucket_np(rel * dl)
            for h in range(6):
                vals = rel_bias[bk, h]
                out[:, (br * 6 + h) * 2 + kt, :] = np.where(valid, vals, np.float32(NEGV))
    return out

def pool_consts(pool_w, pool_scale, first, last, NTl):
    bd = np.zeros((2, 128, 128), np.float32)
    for g in range(4):
        c, hf = divmod(g, 2)
        bd[c, 64 * hf:64 * hf + 64, 64 * hf:64 * hf + 64] = pool_w[g]
    psc = np.ascontiguousarray(pool_scale.reshape(2, 128).T)
    wins = (2, 4, 8, 16)
    invw = np.zeros((128, 2), np.float32); edge = np.ones((128, 2, 16), np.float32)
    for g in range(4):
        c, hf = divmod(g, 2); W = wins[g]
        invw[64 * hf:64 * hf + 64, c] = 1.0 / W
        for t in range(8):
            if first:
                cnt = min(t + W - W // 2, 10 ** 9) - max(t - W // 2, 0)
                edge[64 * hf:64 * hf + 64, c, t] = W / cnt
            if last:
                tt = NTl - 8 + t
                cnt = min(tt + W - W // 2, NTl) - (tt - W // 2)
                edge[64 * hf:64 * hf + 64, c, 8 + t] = W / cnt
    return bd, psc, invw, edge

def kmask_tiles(valid_ext, NTl, HL=1024):
    SB = 2048
    cols = []
    for sb in range(NTl // SB):
        t0 = sb * SB
        for br, dl in enumerate((1, 4, 16)):
            nb = SB // (128 * dl)
            for r in range(dl):
                for b in range(nb):
                    for kt in range(2):
                        e0 = t0 + HL + r + dl * (128 * b - 64 + 128 * kt)
                        idx = e0 + dl * np.arange(128)
                        cols.append(np.where(valid_ext[idx], 0.0, NEGV).astype(np.float32))
    return np.ascontiguousarray(np.stack(cols, axis=1))

def rwkv_consts(rwkv_conv, key_k, key_a, bonus_rk, gn_gain, gn_bias, decay_w0, iclr_a0):
    convw = np.zeros((64, 6, 3, 3), np.float32)
    for h in range(6):
        for w3 in range(3):
            ch = w3 * 384 + h * 64
            convw[:, h, w3, :] = rwkv_conv[:, ch:ch + 64].T
    hp = np.stack([a.reshape(6, 64).T for a in (key_k, key_a, bonus_rk, gn_gain, gn_bias)], axis=-1).astype(np.float32)
    dirp = np.zeros((64, 6, 2, 2), np.float32)
    for d in range(2):
        dirp[:, :, d, 0] = decay_w0[d].reshape(6, 64).T
        dirp[:, :, d, 1] = iclr_a0[d].reshape(6, 64).T
    t = np.arange(128)[:, None]; s = np.arange(128)[None, :]
    maskA = np.zeros((128, 2, 128), np.float32)
    maskA[:, 0, :] = (s < t); maskA[:, 1, :] = (s > t)
    maskT = np.zeros((128, 2, 256), np.float32)
    ss = np.arange(128)[:, None]; tt = np.arange(128)[None, :]
    maskT[:, 0, 0:128] = (ss < tt); maskT[:, 0, 128:256] = (ss <= tt)
    maskT[:, 1, 0:128] = (ss > tt); maskT[:, 1, 128:256] = (ss >= tt)
    mT = np.zeros((128, 2, 640), np.float32)
    mT[:, :, 0:256] = maskT
    for d in range(2):
        A = maskA[:, d, :]
        same = (t // 64) == (s // 64)
        Abd = A * same; A21 = A * (~same)
        mT[:, d, 256:384] = Abd
        mT[:, d, 384:512] = Abd.T
        mT[:, d, 512:640] = A21.T
    maskT = mT
    return np.ascontiguousarray(convw), np.ascontiguousarray(hp), dirp, maskA, maskT


from concourse.bass_utils import run_bass_kernel_spmd

LAYER_KEYS = ("norm_mix_pre", "norm_mix_post", "norm_ffn_pre", "norm_ffn_post", "w_in", "w_out", "pool_w", "pool_scale", "rwkv_conv",
              "decay_w0", "decay_up", "iclr_a0", "iclr_up", "gate_up", "key_k", "key_a", "bonus_rk", "gn_gain", "gn_bias", "w_ff_in", "w_ff_out")


def build_L2(NTl):
    NEl = NTl + 2 * HL
    nc = bass.Bass("TRN2", target_bir_lowering=False)
    with ExitStack() as es:
        P = Prog(nc, es)
        I_ = lambda n, s, dt=F32: P.dram(n, s, dt, kind="ExternalInput")
        O_ = lambda n, s, dt=F32: P.dram(n, s, dt, kind="ExternalOutput")
        xT = I_("xT", [8, 128, NEl]); gain = I_("gain", [128, 8]); w_in = I_("w_in", [8, 128, INW])
        poolw = I_("poolw", [2, 128, 128]); pscale = I_("pscale", [128, 2]); invw = I_("invw", [128, 2]); edge = I_("edge", [128, 2, 16])
        abias = I_("abias", [128, 36, 128]); kmask = I_("kmask", [128, (NTl // 2048) * 96])
        convw = I_("convw", [64, 6, 3, 3]); hp = I_("hp", [64, 6, 5]); dirp = I_("dirp", [64, 6, 2, 2])
        dup = I_("dup", [2, 64, 384]); iup = I_("iup", [2, 64, 384]); gup = I_("gup", [128, 384])
        maskA = I_("maskA", [128, 2, 128]); maskT = I_("maskT", [128, 2, 640])
        PT = P.dram("PT", [22, 128, NEl], F32)
        Vt = P.dram("Vt", [NEl, 384], BF16)
        YP = O_("YP", [2, 128, NTl]); YA = O_("YA", [6, 64, NTl])
        YS = O_("YS", [6, 64, NTl]); QP = O_("QP", [2, 6, 64, NTl], BF16)
        BON = O_("BON", [6, 64, NTl]); GT = O_("GT", [6, 64, NTl]); HEND = O_("HEND", [2, 6, 64, 128])
        phase_A(P, es, xT, gain, w_in, PT, Vt, NEl // TS)
        phase_pool(P, es, PT, poolw, pscale, invw, edge, YP, NTl)
        phase_att(P, es, PT, Vt, abias, kmask, YA, NTl)
        phase_rwkv(P, es, PT, convw, hp, dirp, dup, iup, gup, maskA, maskT, YS, QP, BON, GT, HEND, NTl)
        P.barrier_all_dma("sp")
        P.emit()
    return nc


def build_L3(NTl):
    nc = bass.Bass("TRN2", target_bir_lowering=False)
    with ExitStack() as es:
        P = Prog(nc, es)
        I_ = lambda n, s, dt=F32: P.dram(n, s, dt, kind="ExternalInput")
        O_ = lambda n, s, dt=F32: P.dram(n, s, dt, kind="ExternalOutput")
        xT = I_("xT", [8, 128, NTl])
        YP = I_("YP", [2, 128, NTl]); YA = I_("YA", [6, 64, NTl])
        YS = I_("YS", [6, 64, NTl]); QP = I_("QP", [2, 6, 64, NTl], BF16)
        BON = I_("BON", [6, 64, NTl]); GT = I_("GT", [6, 64, NTl]); PRED = I_("PRED", [2, 6, 3, 64, 128])
        hp = I_("hp", [64, 6, 5])
        wout = I_("wout", [1024, 1024]); gpost = I_("gpost", [128, 8]); gffn = I_("gffn", [128, 8]); gpost2 = I_("gpost2", [128, 8])
        wfi = I_("wfi", [8, 128, 4096]); wfo = I_("wfo", [32, 128, 1024])
        YR = O_("YR", [6, 64, NTl])
        XO = O_("XO", [8, 128, NTl])
        phase_fin(P, es, YS, QP, BON, GT, PRED, hp, YR, NTl)
        phase_E(P, es, xT, YP, YR, YA, wout, gpost, gffn, gpost2, wfi, wfo, XO, NTl, 0)
        P.barrier_all_dma("sp")
        P.emit()
    return nc


_PROGS = {}
_DBG = []


def _prog(kind, NTl):
    key = (kind, NTl)
    if key not in _PROGS:
        _PROGS[key] = build_L2(NTl) if kind == "L2" else build_L3(NTl)
    return _PROGS[key]


def g8(v):
    return np.ascontiguousarray(np.asarray(v, np.float32).reshape(8, 128).T)


def run_trunk(seqs, inputs, NTl):
    segs = []
    for si, x in enumerate(seqs):
        n = x.shape[0] // NTl
        for s in range(n):
            segs.append((si, s, n))
    ncores = len(segs)
    NEl = NTl + 2 * HL
    depth = inputs["w_in"].shape[0]
    X = [np.ascontiguousarray(seqs[si][s * NTl:(s + 1) * NTl].T.reshape(8, 128, NTl)) for (si, s, n) in segs]
    abias = att_bias_tiles(np.asarray(inputs["rel_bias"], np.float32))
    for l in range(depth):
        L = {k: np.asarray(inputs[k][l], np.float32) for k in LAYER_KEYS}
        convw, hp, dirp, maskA, maskT = rwkv_consts(L["rwkv_conv"], L["key_k"], L["key_a"], L["bonus_rk"], L["gn_gain"], L["gn_bias"], L["decay_w0"], L["iclr_a0"])
        in2 = []
        for ci, (si, s, n) in enumerate(segs):
            xe = np.zeros((8, 128, NEl), np.float32)
            xe[:, :, HL:HL + NTl] = X[ci]
            valid = np.zeros(NEl, bool); valid[HL:HL + NTl] = True
            if s > 0:
                xe[:, :, 0:HL] = X[ci - 1][:, :, NTl - HL:NTl]; valid[0:HL] = True
            if s < n - 1:
                xe[:, :, HL + NTl:] = X[ci + 1][:, :, 0:HL]; valid[HL + NTl:] = True
            bd, psc, iw, ed = pool_consts(L["pool_w"], L["pool_scale"], s == 0, s == n - 1, NTl)
            in2.append({"xT": xe, "gain": g8(L["norm_mix_pre"]), "w_in": np.ascontiguousarray(L["w_in"].reshape(8, 128, INW)),
                        "poolw": bd, "pscale": psc, "invw": iw, "edge": ed, "abias": abias, "kmask": kmask_tiles(valid, NTl),
                        "convw": convw, "hp": hp, "dirp": dirp, "dup": np.ascontiguousarray(L["decay_up"]), "iup": np.ascontiguousarray(L["iclr_up"]),
                        "gup": np.ascontiguousarray(L["gate_up"]), "maskA": maskA, "maskT": maskT})
        r2 = run_bass_kernel_spmd(_prog("L2", NTl), in2, core_ids=list(range(ncores))).results
        in3 = []
        for ci, (si, s, n) in enumerate(segs):
            pred = np.zeros((2, 6, 3, 64, 128), np.float32)
            fw_pred = [ci - s + j for j in range(s)]
            bw_pred = [ci - s + j for j in range(n - 1, s, -1)]
            for d, plist in ((0, fw_pred), (1, bw_pred)):
                off = 3 - len(plist)
                for k, cj in enumerate(plist):
                    he = np.asarray(r2[cj]["HEND"], np.float32)[d]
                    pred[d, :, off + k, :, 0:64] = np.transpose(he[:, :, 64:128], (0, 2, 1))
                    pred[d, :, off + k, :, 64:128] = he[:, :, 0:64]
            o = r2[ci]
            in3.append({"xT": X[ci], "YP": o["YP"], "YA": o["YA"], "YS": o["YS"], "QP": o["QP"], "BON": o["BON"], "GT": o["GT"], "PRED": pred,
                        "hp": hp, "wout": np.ascontiguousarray(L["w_out"]), "gpost": g8(L["norm_mix_post"]), "gffn": g8(L["norm_ffn_pre"]),
                        "gpost2": g8(L["norm_ffn_post"]), "wfi": np.ascontiguousarray(L["w_ff_in"].reshape(8, 128, 4096)),
                        "wfo": np.ascontiguousarray(L["w_ff_out"].reshape(32, 128, 1024))})
        r3 = run_bass_kernel_spmd(_prog("L3", NTl), in3, core_ids=list(range(ncores))).results
        X = [np.asarray(r3[ci]["XO"], np.float32) for ci in range(ncores)]
        _DBG.append(([dict(r) for r in r2], [x.copy() for x in X], [np.asarray(r3[ci]["YR"]) for ci in range(ncores)]))
    outs = []
    ci = 0
    for si, x in enumerate(seqs):
        n = x.shape[0] // NTl
        ys = [X[ci + s].reshape(1024, NTl).T for s in range(n)]
        ci += n
        outs.append(np.concatenate(ys, axis=0))
    return outs


def kernel_unfused(**inputs):
    xp = np.asarray(inputs["x_prompt"], np.float32)
    xs = np.asarray(inputs["x_sample"], np.float32)
    seqs = [xp[b] for b in range(xp.shape[0])] + [xs[b] for b in range(xs.shape[0])]
    outs = run_trunk(seqs, inputs, NT)
    nb = xp.shape[0]
    y_prompt = np.stack(outs[:nb], axis=0).astype(np.float32)
    y_sample = np.stack(outs[nb:], axis=0).astype(np.float32)
    return (y_prompt, y_sample)


def phase_pred(P, es_outer, HENDs, vflag_d, s, PRED, nseg):
    with ExitStack() as es:
        P.es = es
        identf = P.sb([128, 128], F32, "identf5"); identb = P.sb([128, 128], BF16, "identb5")
        make_ident(P, identf, identb)
        vf = P.sb([64, 2 * nseg * 3], F32, "vf")
        P.dma("sp", vf[:], vflag_d[:], reads=[vflag_d], writes=[vf])
        zero = P.sb([64, 128], F32, "zero5")
        P.o("pool", "memset", [], [zero], zero[:], 0.0)
        hep = rot([P.sb([64, 128], F32, f"he{i}") for i in range(3)])
        pdp = rot([P.sb([64, 128], F32, f"pd5{i}") for i in range(3)])
        pp = rot([P.ps([128, 512], F32, f"pp{i}") for i in range(2)])
        for d in range(2):
            js = list(range(s)) if d == 0 else list(range(nseg - 1, s, -1))
            off = 3 - len(js)
            for h in range(6):
                for slot in range(off):
                    P.dma("sp", PRED[d, h, slot, :, :], zero[:], reads=[zero], writes=[PRED])
                for k, j in enumerate(js):
                    he = hep(); pd = pdp(); p = pp()
                    fcol = (d * nseg + s) * 3 + k
                    P.dma("sp", he[:], HENDs[j][d, h, :, :], reads=[HENDs[j]], writes=[he])
                    P.o("pe", "matmul", [he, identf], [p], p[0:64, 0:64], lhsT=he[:, 64:128], rhs=identf[0:64, 0:64], start=True, stop=True)
                    P.o("dve", "tensor_scalar", [p, vf], [pd], out=pd[:, 0:64], in0=p[0:64, 0:64], scalar1=vf[:, fcol:fcol + 1], scalar2=0.0, op0=ALU.mult, op1=ALU.add)
                    P.o("pool", "tensor_scalar", [he, vf], [pd], out=pd[:, 64:128], in0=he[:, 0:64], scalar1=vf[:, fcol:fcol + 1], scalar2=0.0, op0=ALU.mult, op1=ALU.add)
                    P.dma("sp", PRED[d, h, off + k, :, :], pd[:], reads=[pd], writes=[PRED])
        P.phase_barrier()
    P.es = es_outer


NSEG = 4
W_NAMES = ("gain", "w_in", "poolw", "pscale", "invw", "convw", "hp", "dirp", "dup", "iup", "gup",
           "wout", "gpost", "gffn", "gpost2", "wfi", "wfo")
W_SHAPES = {"gain": [128, 8], "w_in": [8, 128, INW], "poolw": [2, 128, 128], "pscale": [128, 2], "invw": [128, 2],
            "convw": [64, 6, 3, 3], "hp": [64, 6, 5], "dirp": [64, 6, 2, 2], "dup": [2, 64, 384], "iup": [2, 64, 384], "gup": [128, 384],
            "wout": [1024, 1024], "gpost": [128, 8], "gffn": [128, 8], "gpost2": [128, 8], "wfi": [8, 128, 4096], "wfo": [32, 128, 1024]}


def build_fused(NTl, depth=2, nseg=NSEG):
    NEl = NTl + 2 * HL
    nc = bass.Bass("TRN2", target_bir_lowering=False)
    with ExitStack() as es:
        P = Prog(nc, es)
        I_ = lambda n, s, dt=F32: P.dram(n, s, dt, kind="ExternalInput")
        O_ = lambda n, s, dt=F32: P.dram(n, s, dt, kind="ExternalOutput")
        D_ = lambda n, s, dt=F32: P.dram(n, s, dt)
        def views(buf, n):
            return [Buf(buf.t[i], f"{buf.name}_{i}") for i in range(n)]
        X = [I_(f"x{s}", [8, 128, NTl]) for s in range(nseg)]
        Wl = [{n: I_(f"{n}_l{l}", W_SHAPES[n]) for n in W_NAMES} for l in range(depth)]
        abias = I_("abias", [128, 36, 128])
        maskA = I_("maskA", [128, 2, 128]); maskT = I_("maskT", [128, 2, 640])
        kmask = [I_(f"kmask{s}", [128, (NTl // 2048) * 96]) for s in range(nseg)]
        edge = [I_(f"edge{s}", [128, 2, 16]) for s in range(nseg)]
        hflag = [I_(f"hflag{s}", [128, 2]) for s in range(nseg)]
        kmask_o = I_("kmask_o", [128, (NTl // 2048) * 96]); edge_o = I_("edge_o", [128, 2, 16]); hflag_o = I_("hflag_o", [128, 2])
        vflag = I_("vflag", [64, 2 * nseg * 3])
        XOUT = O_("xo", [8, 128, NTl])
        XE_all = D_("XEall", [nseg, 8, 128, NEl]); XE = views(XE_all, nseg)
        XMID = [[D_(f"XM{l}_{s}", [8, 128, NTl]) for s in range(nseg)] for l in range(depth - 1)]
        PT = D_("PT", [22, 128, NEl]); Vt = D_("Vt", [NEl, 384], BF16)
        YP = [D_(f"YP{s}", [2, 128, NTl]) for s in range(nseg)]; YA = [D_(f"YA{s}", [6, 64, NTl]) for s in range(nseg)]
        YS_all = D_("YSall", [nseg, 6, 64, NTl]); YS = views(YS_all, nseg)
        QP_all = D_("QPall", [nseg, 2, 6, 64, NTl], BF16); QP = views(QP_all, nseg)
        BON_all = D_("BONall", [nseg, 6, 64, NTl]); BON = views(BON_all, nseg)
        GT_all = D_("GTall", [nseg, 6, 64, NTl]); GT = views(GT_all, nseg)
        HEND = [D_(f"HEND{s}", [2, 6, 64, 128]) for s in range(nseg)]
        PRED_all = D_("PREDall", [nseg, 2, 6, 3, 64, 128]); PREDv = views(PRED_all, nseg)
        PRED = D_("PRED", [2, 6, 3, 64, 128]); YR = D_("YR", [6, 64, NTl])
        XE_o = D_("XEo", [8, 128, NEl]); YS_o = D_("YSo", [6, 64, NTl]); QP_o = D_("QPo", [2, 6, 64, NTl], BF16)
        BON_o = D_("BONo", [6, 64, NTl]); GT_o = D_("GTo", [6, 64, NTl]); YP_o = D_("YPo", [2, 128, NTl]); YA_o = D_("YAo", [6, 64, NTl])

        def dyn_copy(dst, src_all, srcviews, pat_in, pat_out=None):
            def f(e):
                own = e.partition_id() % nseg
                src = src_all.t[tuple([bass.ds(own, 1)] + [slice(None)] * (len(src_all.t.shape) - 1))].rearrange(pat_in)
                d_ = dst.t if pat_out is None else dst.t.rearrange(pat_out)
                return e.dma_start(out=d_, in_=src)
            P.custom("sp", f, 16, reads=list(srcviews), writes=[dst])

        for l in range(depth):
            Wd = Wl[l]
            last = (l == depth - 1)
            src = X if l == 0 else XMID[l - 1]
            for s in range(nseg):
                ln = src[s - 1] if s > 0 else src[s]
                rn = src[s + 1] if s < nseg - 1 else src[s]
                P.dma("sp", XE[s][:, :, HL:HL + NTl], src[s][:], reads=[src[s]], writes=[XE[s]])
                P.dma("sp", XE[s][:, :, 0:HL], ln[:, :, NTl - HL:NTl], reads=[ln], writes=[XE[s]])
                P.dma("sp", XE[s][:, :, HL + NTl:NEl], rn[:, :, 0:HL], reads=[rn], writes=[XE[s]])
            P.phase_barrier()
            if not last:
                for s in range(nseg):
                    phase_A(P, es, XE[s], Wd["gain"], Wd["w_in"], PT, Vt, NEl // TS, hflag_d=hflag[s])
                    phase_pool(P, es, PT, Wd["poolw"], Wd["pscale"], Wd["invw"], edge[s], YP[s], NTl)
                    phase_att(P, es, PT, Vt, abias, kmask[s], YA[s], NTl)
                    phase_rwkv(P, es, PT, Wd["convw"], Wd["hp"], Wd["dirp"], Wd["dup"], Wd["iup"], Wd["gup"], maskA, maskT,
                               YS[s], QP[s], BON[s], GT[s], HEND[s], NTl)
                for s in range(nseg):
                    phase_pred(P, es, HEND, vflag, s, PRED, nseg)
                    phase_fin(P, es, YS[s], QP[s], BON[s], GT[s], PRED, Wd["hp"], YR, NTl)
                    phase_E(P, es, XE[s], YP[s], YR, YA[s], Wd["wout"], Wd["gpost"], Wd["gffn"], Wd["gpost2"], Wd["wfi"], Wd["wfo"], XMID[l][s], NTl, HL)
            else:
                nti = NEl // TS
                for s in range(nseg):
                    phase_A(P, es, XE[s], Wd["gain"], Wd["w_in"], PT, Vt, nti, hflag_d=hflag[s], cols=range(2, 13), do_v=False,
                            tiles=range(HL // TS - 1, nti - HL // TS + 1))
                    phase_rwkv(P, es, PT, Wd["convw"], Wd["hp"], Wd["dirp"], Wd["dup"], Wd["iup"], Wd["gup"], maskA, maskT,
                               YS[s], QP[s], BON[s], GT[s], HEND[s], NTl)
                    phase_pred(P, es, HEND, vflag, s, PREDv[s], nseg) if False else None
                for s in range(nseg):
                    phase_pred(P, es, HEND, vflag, s, PREDv[s], nseg)
                dyn_copy(XE_o, XE_all, XE, "o c p t -> (o c) p t")
                dyn_copy(YS_o, YS_all, YS, "o h p t -> (o h) p t")
                dyn_copy(QP_o, QP_all, QP, "o d h p t -> (o d) h p t")
                dyn_copy(BON_o, BON_all, BON, "o h p t -> (o h) p t")
                dyn_copy(GT_o, GT_all, GT, "o h p t -> (o h) p t")
                dyn_copy(PRED, PRED_all, PREDv, "o d h s p t -> (o d h s) p t", "d h s p t -> (d h s) p t")
                P.phase_barrier()
                phase_A(P, es, XE_o, Wd["gain"], Wd["w_in"], PT, Vt, nti, hflag_d=hflag_o)
                phase_pool(P, es, PT, Wd["poolw"], Wd["pscale"], Wd["invw"], edge_o, YP_o, NTl)
                phase_att(P, es, PT, Vt, abias, kmask_o, YA_o, NTl)
                phase_fin(P, es, YS_o, QP_o, BON_o, GT_o, PRED, Wd["hp"], YR, NTl)
                phase_E(P, es, XE_o, YP_o, YR, YA_o, Wd["wout"], Wd["gpost"], Wd["gffn"], Wd["gpost2"], Wd["wfi"], Wd["wfo"], XOUT, NTl, HL)
        P.barrier_all_dma("sp")
        P.emit()
    return nc


def fused_inputs(groups, inputs, NTl, nseg=NSEG, own=None):
    depth = inputs["w_in"].shape[0]
    abias = att_bias_tiles(np.asarray(inputs["rel_bias"], np.float32))
    wmaps = {}
    for l in range(depth):
        L = {k: np.asarray(inputs[k][l], np.float32) for k in LAYER_KEYS}
        convw, hp, dirp, maskA, maskT = rwkv_consts(L["rwkv_conv"], L["key_k"], L["key_a"], L["bonus_rk"], L["gn_gain"], L["gn_bias"], L["decay_w0"], L["iclr_a0"])
        bd, psc, iw, _ = pool_consts(L["pool_w"], L["pool_scale"], False, False, NTl)
        vals = {"gain": g8(L["norm_mix_pre"]), "w_in": np.ascontiguousarray(L["w_in"].reshape(8, 128, INW)), "poolw": bd, "pscale": psc, "invw": iw,
                "convw": convw, "hp": hp, "dirp": dirp, "dup": np.ascontiguousarray(L["decay_up"]), "iup": np.ascontiguousarray(L["iclr_up"]),
                "gup": np.ascontiguousarray(L["gate_up"]), "wout": np.ascontiguousarray(L["w_out"]), "gpost": g8(L["norm_mix_post"]),
                "gffn": g8(L["norm_ffn_pre"]), "gpost2": g8(L["norm_ffn_post"]), "wfi": np.ascontiguousarray(L["w_ff_in"].reshape(8, 128, 4096)),
                "wfo": np.ascontiguousarray(L["w_ff_out"].reshape(32, 128, 1024))}
        for n in W_NAMES:
            wmaps[f"{n}_l{l}"] = vals[n]
    NEl = NTl + 2 * HL
    in_maps = []
    for ci, seqs in enumerate(groups):
        m = dict(wmaps)
        m["abias"] = abias; m["maskA"] = maskA; m["maskT"] = maskT
        segs = []
        for x in seqs:
            n = x.shape[0] // NTl
            for q in range(n):
                segs.append((x, q, n))
        assert len(segs) == nseg
        cont = [0.0] * nseg
        for j, (x, q, n) in enumerate(segs):
            cont[j] = 1.0 if q > 0 else 0.0
        vflag = np.zeros((64, 2 * nseg * 3), np.float32)
        for s, (x, q, n) in enumerate(segs):
            m[f"x{s}"] = np.ascontiguousarray(x[q * NTl:(q + 1) * NTl].T.reshape(8, 128, NTl))
            valid = np.zeros(NEl, bool); valid[HL:HL + NTl] = True
            if q > 0:
                valid[0:HL] = True
            if q < n - 1:
                valid[HL + NTl:] = True
            m[f"kmask{s}"] = kmask_tiles(valid, NTl)
            _, _, _, ed = pool_consts(np.zeros((4, 64, 64), np.float32), np.zeros(256, np.float32), q == 0, q == n - 1, NTl)
            m[f"edge{s}"] = ed
            hf = np.zeros((128, 2), np.float32); hf[:, 0] = 1.0 if q > 0 else 0.0; hf[:, 1] = 1.0 if q < n - 1 else 0.0
            m[f"hflag{s}"] = hf
            for d in range(2):
                js = list(range(s)) if d == 0 else list(range(nseg - 1, s, -1))
                for k, j in enumerate(js):
                    if d == 0:
                        ok = all(cont[i] == 1.0 for i in range(j + 1, s + 1))
                    else:
                        ok = all(cont[i] == 1.0 for i in range(s + 1, j + 1))
                    vflag[:, (d * nseg + s) * 3 + k] = 1.0 if ok else 0.0
        m["vflag"] = vflag
        o = own[ci] if own is not None else 0
        m["kmask_o"] = m[f"kmask{o}"]; m["edge_o"] = m[f"edge{o}"]; m["hflag_o"] = m[f"hflag{o}"]
        in_maps.append(m)
    return in_maps


def kernel_fused(inputs, NTl=None, groups=None, own=None):
    NTl = NTl or NT
    xp = np.asarray(inputs["x_prompt"], np.float32)
    xs = np.asarray(inputs["x_sample"], np.float32)
    if groups is None:
        gp = [xp[b] for b in range(xp.shape[0])]
        gs = [xs[b] for b in range(xs.shape[0])]
        groups = [gp] * 4 + [gs] * 4
        own = [0, 1, 2, 3, 0, 1, 2, 3]
    key = ("F", NTl)
    if key not in _PROGS:
        _PROGS[key] = build_fused(NTl, depth=inputs["w_in"].shape[0])
    in_maps = fused_inputs(groups, inputs, NTl, own=own)
    res = run_bass_kernel_spmd(_PROGS[key], in_maps, core_ids=list(range(len(groups)))).results
    return [np.asarray(res[c]["xo"], np.float32).reshape(1024, NTl).T for c in range(len(groups))]


def kernel(**inputs):
    xp = np.asarray(inputs["x_prompt"], np.float32)
    xs = np.asarray(inputs["x_sample"], np.float32)
    outs = kernel_fused(inputs)
    y_prompt = np.stack([np.concatenate(outs[0:2], axis=0), np.concatenate(outs[2:4], axis=0)], axis=0).astype(np.float32)
    y_sample = np.concatenate(outs[4:8], axis=0)[None].astype(np.float32)
    return (y_prompt, y_sample)
```

```python
import numpy as np
import concourse.bass as bass
import concourse.mybir as mybir
from contextlib import ExitStack

F32 = mybir.dt.float32
BF16 = mybir.dt.bfloat16
I32 = mybir.dt.int32
ALU = mybir.AluOpType
AF = mybir.ActivationFunctionType
AX = mybir.AxisListType

SEM_LIMIT = 30000
NDMASEM = 8


class Buf:
    __slots__ = ("t", "name", "last_w", "readers")

    def __init__(self, t, name):
        self.t = t
        self.name = name
        self.last_w = None
        self.readers = []

    def __getitem__(self, idx):
        return self.t[idx]


class Prog:
    ENG = ("pe", "act", "dve", "pool", "sp")

    def __init__(self, nc, es):
        self.nc = nc
        self.es = es
        self.streams = {e: [] for e in self.ENG}
        self.cnt = {e: 0 for e in self.ENG}
        self.sem = {}
        self.semlist = {e: [] for e in self.ENG}
        self.waited = {e: {} for e in self.ENG}
        self.dma_sems = {}
        self.dma_n = {}
        for e in ("sp", "pool", "act"):
            self.dma_sems[e] = [es.enter_context(nc.semaphore(f"dq_{e}_{i}")) for i in range(NDMASEM)]
            self.dma_n[e] = 0
        for e in ("pe", "act", "dve", "pool"):
            self._newsem(e)
        self.nbuf = 0

    def _newsem(self, e):
        s = self.es.enter_context(self.nc.semaphore(f"s_{e}_{len(self.semlist[e])}"))
        self.semlist[e].append(s)
        self.sem[e] = s
        self.cnt[e] = 0

    def sb(self, shape, dt, name=None):
        self.nbuf += 1
        name = name or f"sb{self.nbuf}"
        t = self.es.enter_context(self.nc.sbuf_tensor(f"{name}_{self.nbuf}", list(shape), dt))
        return Buf(t, name)

    def ps(self, shape, dt=F32, name=None):
        self.nbuf += 1
        name = name or f"ps{self.nbuf}"
        t = self.es.enter_context(self.nc.psum_tensor(f"{name}_{self.nbuf}", list(shape), dt))
        return Buf(t, name)

    def ps_multi(self, n, shape, dt=F32, name=None):
        self.nbuf += 1
        name = name or f"psm{self.nbuf}"
        t = self.es.enter_context(self.nc.psum_tensor(f"{name}_{self.nbuf}", [shape[0], n, shape[1]], dt))
        return [Buf(t[:, i, :], f"{name}{i}") for i in range(n)]

    def dram(self, name, shape, dt, kind="Internal", addr_space="Local"):
        t = self.nc.dram_tensor(name, list(shape), dt, kind=kind, addr_space=addr_space)
        return Buf(t.ap(), name)

    def _wait(self, eng, ev):
        if ev is None:
            return
        sem, val, src = ev
        if src == "pe" and eng == "pe":
            return
        w = self.waited[eng]
        key = id(sem)
        if w.get(key, 0) >= val:
            return
        w[key] = val
        self.streams[eng].append(("wait", sem, val))

    def _deps(self, eng, reads, writes):
        for b in reads:
            self._wait(eng, b.last_w)
        for b in writes:
            self._wait(eng, b.last_w)
            for ev in b.readers:
                self._wait(eng, ev)

    def _mark(self, ev, reads, writes):
        for b in writes:
            b.last_w = ev
            b.readers = []
        for b in reads:
            if b in writes:
                continue
            b.readers = [r for r in b.readers if not (r[0] is ev[0])] + [ev]

    def op(self, eng, fn, reads=(), writes=()):
        self._deps(eng, reads, writes)
        if self.cnt[eng] >= SEM_LIMIT:
            self._newsem(eng)
        self.cnt[eng] += 1
        ev = (self.sem[eng], self.cnt[eng], eng)
        self.streams[eng].append(("op", fn, self.sem[eng], 1))
        self._mark(ev, reads, writes)
        return ev

    def o(self, eng, name, reads, writes, *args, **kw):
        return self.op(eng, lambda e: getattr(e, name)(*args, **kw), reads, writes)

    def dma(self, q, out_ap, in_ap, reads=(), writes=(), **kw):
        n = self.dma_n[q]
        self.dma_n[q] += 1
        sem = self.dma_sems[q][n % NDMASEM]
        prev = 16 * (n // NDMASEM)
        if prev > 0:
            self._wait(q, (sem, prev, "dma"))
        self._deps(q, reads, writes)
        ev = (sem, prev + 16, "dma")
        self.streams[q].append(("op", lambda e: e.dma_start(out=out_ap, in_=in_ap, **kw), sem, 16))
        self._mark(ev, reads, writes)
        return ev

    def custom(self, q, fn, inc, reads=(), writes=()):
        n = self.dma_n[q]
        self.dma_n[q] += 1
        sem = self.dma_sems[q][n % NDMASEM]
        prev = 16 * (n // NDMASEM)
        if prev > 0:
            self._wait(q, (sem, prev, "dma"))
        self._deps(q, reads, writes)
        ev = (sem, prev + inc, "dma")
        assert inc == 16
        self.streams[q].append(("op", fn, sem, inc))
        self._mark(ev, reads, writes)
        return ev

    def barrier_all_dma(self, eng="sp"):
        for q in ("sp", "pool", "act"):
            n = self.dma_n[q]
            for i in range(NDMASEM):
                cnt = (n - i + NDMASEM - 1) // NDMASEM if n > i else 0
                if cnt > 0:
                    self._wait(eng, (self.dma_sems[q][i], 16 * cnt, "dma"))

    def phase_barrier(self):
        for eng in self.ENG:
            for f in ("pe", "act", "dve", "pool"):
                if self.cnt[f] > 0 and f != eng:
                    self._wait(eng, (self.sem[f], self.cnt[f], f))
            self.barrier_all_dma(eng)

    def emit(self):
        nc = self.nc
        streams = self.streams
        with nc.Block() as block:
            def run(e, lst):
                for it in lst:
                    if it[0] == "wait":
                        e.wait_ge(it[1], it[2])
                    else:
                        ins = it[1](e)
                        ins.then_inc(it[2], it[3])

            @block.tensor
            def _(e):
                run(e, streams["pe"])

            @block.scalar
            def _(e):
                run(e, streams["act"])

            @block.vector
            def _(e):
                run(e, streams["dve"])

            @block.gpsimd
            def _(e):
                run(e, streams["pool"])

            @block.sync
            def _(e):
                run(e, streams["sp"])


NT = 4096; HL = 1024; NE = NT + 2 * HL; TS = 512
D = 1024; INW = 2816
EPS = 1e-6


def rot(lst):
    st = {"i": 0}
    def nxt():
        b = lst[st["i"] % len(lst)]; st["i"] += 1
        return b
    return nxt


def rms_stats(P, xs, sq, ones_bf, ss_ps, tmp, rstd, eps_t, n_chunks=8, width=TS, dfeat=D):
    P.op("act", lambda e: e.activation(out=sq[:], in_=xs[:], func=AF.Square), reads=[xs], writes=[sq])
    for c in range(n_chunks):
        P.op("pe", lambda e, c=c: e.matmul(ss_ps[:], lhsT=ones_bf[:], rhs=sq[:, c, :], start=(c == 0), stop=(c == n_chunks - 1)),
             reads=[ones_bf, sq], writes=[ss_ps])
    P.op("act", lambda e: e.activation(out=tmp[:], in_=ss_ps[:], func=AF.Sqrt, scale=1.0 / dfeat, bias=eps_t[:]),
         reads=[ss_ps, eps_t], writes=[tmp])
    P.op("dve", lambda e: e.reciprocal(out=rstd[:], in_=tmp[:]), reads=[tmp], writes=[rstd])


def phase_A(P, es_outer, xT, gain_d, w_in_d, PT, Vtok, ntiles, hflag_d=None, cols=None, do_v=True, tiles=None, far_cols=None):
    nc = P.nc
    with ExitStack() as es:
        P.es = es
        W = P.sb([128, 8, INW], BF16, "Win")
        gain = P.sb([128, 8], F32, "gain")
        ones_bf = P.sb([128, 128], BF16, "ones")
        xs2 = [P.sb([128, 8, TS], F32, f"xs{i}") for i in range(2)]
        sq = P.sb([128, 8, TS], BF16, "sq")
        hT = P.sb([128, 8, TS], BF16, "hT")
        tmp = P.sb([128, TS], F32, "tmp"); rstd = P.sb([128, TS], F32, "rstd")
        stg = rot([P.sb([128, TS], F32, f"stg{i}") for i in range(4)])
        vst = rot([P.sb([128, 384], BF16, f"vst{i}") for i in range(2)])
        ss_ps = P.ps([128, TS], F32, "ssps")
        mm = rot([P.ps([128, TS], F32, f"mm{i}") for i in range(4)])
        vps = rot([P.ps([128, 384], F32, f"vps{i}") for i in range(2)])
        for c in range(8):
            P.dma("pool", W[:, c, :], w_in_d[c, :, :], reads=[w_in_d], writes=[W])
        P.dma("sp", gain[:], gain_d[:], reads=[gain_d], writes=[gain])
        P.op("dve", lambda e: e.memset(ones_bf[:], 1.0), writes=[ones_bf])
        eps_t = P.sb([128, 1], F32, "eps")
        P.op("dve", lambda e: e.memset(eps_t[:], EPS), writes=[eps_t])
        if hflag_d is not None:
            hflag = P.sb([128, 2], F32, "hflag")
            P.dma("sp", hflag[:], hflag_d[:], reads=[hflag_d], writes=[hflag])
        def load(j):
            P.dma("sp", xs2[j % 2][:], xT[:, :, j * TS:(j + 1) * TS].rearrange("c p t -> p c t"), reads=[xT], writes=[xs2[j % 2]])
            if hflag_d is not None and (j < HL // TS or j >= ntiles - HL // TS):
                col = 0 if j < HL // TS else 1
                P.o("pool", "tensor_scalar", [xs2[j % 2], hflag], [xs2[j % 2]], out=xs2[j % 2][:], in0=xs2[j % 2][:], scalar1=hflag[:, col:col + 1], scalar2=0.0, op0=ALU.mult, op1=ALU.add)
        tile_list = list(range(ntiles)) if tiles is None else list(tiles)
        col_list = list(range(INW // 128)) if cols is None else list(cols)
        load(tile_list[0])
        evi = 0
        for ji, j in enumerate(tile_list):
            if ji + 1 < len(tile_list):
                load(tile_list[ji + 1])
            xs = xs2[j % 2]
            rms_stats(P, xs, sq, ones_bf, ss_ps, tmp, rstd, eps_t)
            for c in range(8):
                P.op("dve", lambda e, c=c, xs=xs: e.scalar_tensor_tensor(out=hT[:, c, :], in0=xs[:, c, :], scalar=gain[:, c:c + 1], in1=rstd[:], op0=ALU.mult, op1=ALU.mult),
                     reads=[xs, gain, rstd], writes=[hT])
            is_far = far_cols is not None and (j < HL // TS - 1 or j > ntiles - HL // TS)
            for m in (list(far_cols) if is_far else col_list):
                ps = mm()
                for c in range(8):
                    P.op("pe", lambda e, c=c, m=m, ps=ps: e.matmul(ps[:], lhsT=W[:, c, m * 128:(m + 1) * 128], rhs=hT[:, c, :], start=(c == 0), stop=(c == 7)),
                         reads=[W, hT], writes=[ps])
                st = stg()
                if evi % 2 == 0:
                    P.op("act", lambda e, ps=ps, st=st: e.activation(out=st[:], in_=ps[:], func=AF.Copy), reads=[ps], writes=[st])
                else:
                    P.op("dve", lambda e, ps=ps, st=st: e.tensor_copy(out=st[:], in_=ps[:]), reads=[ps], writes=[st])
                evi += 1
                P.dma("pool", PT[m, :, j * TS:(j + 1) * TS], st[:], reads=[st], writes=[PT])
            for s in (range(TS // 128) if do_v else []):
                ps = vps()
                for c in range(8):
                    P.op("pe", lambda e, c=c, s=s, ps=ps: e.matmul(ps[:], lhsT=hT[:, c, s * 128:(s + 1) * 128], rhs=W[:, c, INW - 384:INW], start=(c == 0), stop=(c == 7)),
                         reads=[W, hT], writes=[ps])
                st = vst()
                P.op("act", lambda e, ps=ps, st=st: e.activation(out=st[:], in_=ps[:], func=AF.Copy), reads=[ps], writes=[st])
                r0 = j * TS + s * 128
                P.dma("pool", Vtok[r0:r0 + 128, :], st[:], reads=[st], writes=[Vtok])
        P.phase_barrier()
    P.es = es_outer


CK = 128
CDEC = 0.6065306597126334
NEG = -1.0e30


def phase_pool(P, es_outer, PT, poolw_d, pscale_d, invw_d, edge_d, YP, NTl):
    with ExitStack() as es:
        P.es = es
        Wn = NTl + 16
        pw = P.sb([128, 2, 128], BF16, "pw")
        psc = P.sb([128, 2], F32, "psc"); invw = P.sb([128, 2], F32, "invw"); edge = P.sb([128, 2, 16], F32, "edge")
        for c in range(2):
            P.dma("pool", pw[:, c, :], poolw_d[c, :, :], reads=[poolw_d], writes=[pw])
        P.dma("sp", psc[:], pscale_d[:], reads=[pscale_d], writes=[psc])
        P.dma("sp", invw[:], invw_d[:], reads=[invw_d], writes=[invw])
        P.dma("sp", edge[:], edge_d[:], reads=[edge_d], writes=[edge])
        U = P.sb([128, Wn], F32, "U"); S1 = P.sb([128, Wn], F32, "S1"); S2 = P.sb([128, Wn], F32, "S2")
        tmp = P.sb([128, NTl], F32, "ptmp")
        pb = P.sb([128, NTl], BF16, "pb")
        mm = rot([P.ps([128, TS], F32, f"pm{i}") for i in range(2)])
        stg = rot([P.sb([128, TS], F32, f"pst{i}") for i in range(2)])
        wins = (2, 4, 8, 16)
        for c in range(2):
            P.dma("sp", U[:], PT[c, :, HL - 8:HL + NTl + 8], reads=[PT], writes=[U])
            for half in range(2):
                W = wins[2 * c + half]
                pr = slice(64 * half, 64 * half + 64)
                src = U; cur = 1; bufs = [S1, S2]; bi = 0
                while cur < W:
                    dst = bufs[bi]; bi ^= 1
                    L = Wn - 2 * cur + 1
                    P.o("dve", "tensor_tensor", [src], [dst], out=dst[pr, 0:L], in0=src[pr, 0:L], in1=src[pr, cur:cur + L], op=ALU.add)
                    src = dst; cur *= 2
                o0 = 8 - W // 2
                P.o("dve", "tensor_scalar", [src, invw], [tmp], out=tmp[pr, :], in0=src[pr, o0:o0 + NTl], scalar1=invw[pr, c:c + 1], scalar2=0.0, op0=ALU.mult, op1=ALU.add)
                P.o("dve", "tensor_tensor", [tmp, edge], [tmp], out=tmp[pr, 0:8], in0=tmp[pr, 0:8], in1=edge[pr, c, 0:8], op=ALU.mult)
                P.o("dve", "tensor_tensor", [tmp, edge], [tmp], out=tmp[pr, NTl - 8:NTl], in0=tmp[pr, NTl - 8:NTl], in1=edge[pr, c, 8:16], op=ALU.mult)
                P.o("dve", "tensor_tensor", [tmp, U], [pb], out=pb[pr, :], in0=tmp[pr, :], in1=U[pr, 8:8 + NTl], op=ALU.subtract)
            for j in range(NTl // TS):
                ps = mm()
                P.o("pe", "matmul", [pw, pb], [ps], ps[:], lhsT=pw[:, c, :], rhs=pb[:, j * TS:(j + 1) * TS], start=True, stop=True)
                st = stg()
                P.o("dve", "tensor_scalar", [ps, psc], [st], out=st[:], in0=ps[:], scalar1=psc[:, c:c + 1], scalar2=0.0, op0=ALU.mult, op1=ALU.add)
                P.dma("sp", YP[c, :, j * TS:(j + 1) * TS], st[:], reads=[st], writes=[YP])
        P.phase_barrier()
    P.es = es_outer


def phase_att(P, es_outer, PT, Vtok, bias_d, kmask_d, YA, NTl):
    SB = 2048
    with ExitStack() as es:
        P.es = es
        bias = P.sb([128, 36, 128], F32, "abias")
        P.dma("sp", bias[:], bias_d[:], reads=[bias_d], writes=[bias])
        qT = [P.sb([64, SB], BF16, f"qT{h}") for h in range(6)]
        kT = [P.sb([64, SB + 2 * HL], BF16, f"kT{h}") for h in range(6)]
        acc = [P.sb([128, SB], F32, f"acc{h}") for h in range(6)]
        vaug = rot([P.sb([128, 6, 128], BF16, f"vaug{i}") for i in range(4)])
        for i in range(4):
            v = vaug()
            P.o("pool", "memset", [], [v], v[:], 1.0)
        nkt = (NTl // SB) * 3 * 16 * 2
        kmall = P.sb([128, nkt], F32, "kmall")
        P.dma("sp", kmall[:], kmask_d[:], reads=[kmask_d], writes=[kmall])
        sps = rot([P.ps([128, 512], F32, f"sps{i}") for i in range(6)])
        opsr = rot([P.ps([128, 512], F32, f"ops{i}") for i in range(2)])
        lg = rot([P.sb([128, 128], F32, f"lg{i}") for i in range(6)])
        pT = rot([P.sb([128, 128], BF16, f"pT{i}") for i in range(6)])
        den = P.sb([64, SB], F32, "den")
        yst = P.sb([64, SB], F32, "yst")
        for sb in range(NTl // SB):
            t0 = sb * SB
            for h in range(6):
                pr = slice(64 * (h % 2), 64 * (h % 2) + 64)
                P.dma("pool", qT[h][:], PT[13 + h // 2, pr, HL + t0:HL + t0 + SB], reads=[PT], writes=[qT[h]])
                P.dma("pool", kT[h][:], PT[16 + h // 2, pr, t0:t0 + SB + 2 * HL], reads=[PT], writes=[kT[h]])
            for br, dl in enumerate((1, 4, 16)):
                nb = SB // (128 * dl)
                for r in range(dl):
                    for b in range(nb):
                        qs = r + dl * 128 * b
                        vas = []; kss = []; kids = []
                        for kt in range(2):
                            ks = HL + r + dl * (128 * b - 64 + 128 * kt)
                            va = vaug()
                            kid = ((sb * 3 + br) * 16 + (r * nb + b)) * 2 + kt
                            e0 = t0 + ks
                            P.dma("sp", va[:, :, 0:64], Vtok[e0:e0 + 127 * dl + 1:dl, :].rearrange("p (h d) -> p h d", h=6), reads=[Vtok], writes=[va])
                            vas.append(va); kss.append(ks); kids.append(kid)
                        def s_pair(h):
                            outp = []
                            for kt in range(2):
                                ks = kss[kt]
                                sp_ = sps()
                                P.o("pe", "matmul", [kT[h], qT[h]], [sp_], sp_[:, 0:128], lhsT=kT[h][:, ks:ks + 127 * dl + 1:dl], rhs=qT[h][:, qs:qs + 127 * dl + 1:dl], start=True, stop=True)
                                l_ = lg()
                                P.o("dve", "scalar_tensor_tensor", [sp_, bias], [l_], out=l_[:], in0=sp_[:, 0:128], scalar=0.125, in1=bias[:, (br * 6 + h) * 2 + kt, :], op0=ALU.mult, op1=ALU.add)
                                p_ = pT()
                                P.o("act", "activation", [l_, kmall], [p_], out=p_[:], in_=l_[:], func=AF.Exp, bias=kmall[:, kids[kt]:kids[kt] + 1], scale=1.0)
                                outp.append(p_)
                            return outp
                        pend = s_pair(0); pend2 = s_pair(1)
                        for h in range(6):
                            nxt = s_pair(h + 2) if h + 2 < 6 else None
                            op_ = opsr()
                            for kt in range(2):
                                P.o("pe", "matmul", [vas[kt], pend[kt]], [op_], op_[:, 0:128], lhsT=vas[kt][:, h, :], rhs=pend[kt][:], start=(kt == 0), stop=(kt == 1))
                            pend = pend2; pend2 = nxt
                            dst = acc[h][:, qs:qs + 127 * dl + 1:dl]
                            if br == 0:
                                if h % 2:
                                    P.o("dve", "tensor_copy", [op_], [acc[h]], out=dst, in_=op_[:, 0:128])
                                else:
                                    P.o("act", "copy", [op_], [acc[h]], out=dst, in_=op_[:, 0:128])
                            else:
                                P.o("dve", "tensor_tensor", [op_, acc[h]], [acc[h]], out=dst, in0=op_[:, 0:128], in1=dst, op=ALU.add)
            for h in range(6):
                P.dma("sp", den[:], acc[h][64:128, :], reads=[acc[h]], writes=[den])
                P.o("dve", "reciprocal", [den], [den], out=den[:], in_=den[:])
                P.o("dve", "tensor_tensor", [acc[h], den], [yst], out=yst[:], in0=acc[h][0:64, :], in1=den[:], op=ALU.mult)
                P.dma("sp", YA[h, :, t0:t0 + SB], yst[:], reads=[yst], writes=[YA])
        P.phase_barrier()
    P.es = es_outer


TL = 512


def make_ident(P, identf, identb):
    P.o("pool", "memset", [], [identf], identf[:], 1.0)
    P.o("pool", "affine_select", [identf], [identf], out=identf[:], in_=identf[:], pattern=[[-1, 128]], compare_op=ALU.is_equal, fill=0.0, base=0, channel_multiplier=1)
    P.o("pool", "tensor_copy", [identf], [identb], out=identb[:], in_=identf[:])


def phase_rwkv(P, es_outer, PT, convw_d, hp_d, dirp_d, dup_d, iup_d, gup_d, maskA_d, maskT_d,
               YS, QP, BON, GT, HEND, NTl):
    nt = NTl // TL
    with ExitStack() as es:
        P.es = es
        identf = P.sb([128, 128], F32, "identf"); identb = P.sb([128, 128], BF16, "identb")
        make_ident(P, identf, identb)
        ones64 = P.sb([64, 64], F32, "ones64")
        P.o("pool", "memset", [], [ones64], ones64[:], 1.0)
        mask0 = P.sb([64, TL], F32, "mask0")
        P.o("pool", "memset", [], [mask0], mask0[:], 1.0)
        P.o("pool", "memset", [mask0], [mask0], mask0[:, 0:TL:128], 0.0)
        convw = P.sb([64, 6, 3, 3], F32, "convw"); hp = P.sb([64, 6, 5], F32, "hp"); dirp = P.sb([64, 6, 2, 2], F32, "dirp")
        omka = P.sb([64, 6], F32, "omka")
        P.dma("sp", convw[:], convw_d[:], reads=[convw_d], writes=[convw])
        P.dma("sp", hp[:], hp_d[:], reads=[hp_d], writes=[hp])
        P.dma("sp", dirp[:], dirp_d[:], reads=[dirp_d], writes=[dirp])
        P.o("dve", "tensor_scalar", [hp], [omka], out=omka[:], in0=hp[:, :, 1], scalar1=-1.0, scalar2=1.0, op0=ALU.mult, op1=ALU.add)
        dup = P.sb([64, 2, 384], BF16, "dup"); iup = P.sb([64, 2, 384], BF16, "iup"); gup = P.sb([128, 384], BF16, "gup")
        for d in range(2):
            P.dma("pool", dup[:, d, :], dup_d[d, :, :], reads=[dup_d], writes=[dup])
            P.dma("pool", iup[:, d, :], iup_d[d, :, :], reads=[iup_d], writes=[iup])
        P.dma("pool", gup[:], gup_d[:], reads=[gup_d], writes=[gup])
        maskA = P.sb([128, 2, 128], F32, "maskA"); maskT = P.sb([128, 2, 640], F32, "maskT")
        P.dma("sp", maskA[:], maskA_d[:], reads=[maskA_d], writes=[maskA])
        P.dma("sp", maskT[:], maskT_d[:], reads=[maskT_d], writes=[maskT])
        tanhw = P.sb([64, NTl], BF16, "tanhw"); alat = P.sb([64, NTl], BF16, "alat"); sgl = P.sb([128, NTl], BF16, "sgl")
        latp = rot([P.sb([64, TL], F32, f"lat{i}") for i in range(1)]); glatp = rot([P.sb([128, TL], F32, f"glat{i}") for i in range(1)])
        P.dma("pool", alat[:], PT[11, 64:128, HL:HL + NTl], reads=[PT], writes=[alat])
        for j in range(nt):
            lat = latp(); glat = glatp()
            P.dma("sp", lat[:], PT[11, 0:64, HL + j * TL:HL + (j + 1) * TL], reads=[PT], writes=[lat])
            P.o("act", "activation", [lat], [tanhw], out=tanhw[:, j * TL:(j + 1) * TL], in_=lat[:], func=AF.Tanh)
            P.dma("sp", glat[:], PT[12, :, HL + j * TL:HL + (j + 1) * TL], reads=[PT], writes=[glat])
            P.o("act", "activation", [glat], [sgl], out=sgl[:, j * TL:(j + 1) * TL], in_=glat[:], func=AF.Sigmoid)
        P.nbuf += 1
        RKV = P.dram(f"RKVs{P.nbuf}", [4, 64, NTl], F32)
        rawp = [rot([P.sb([64, TL + 2], F32, f"raw{w}_{i}") for i in range(1)]) for w in range(3)]
        vbfp = rot([P.sb([64, TL], BF16, f"vbft{i}") for i in range(2)])
        YSa = P.sb([64, NTl], F32, "YSa"); BONa = P.sb([64, NTl], F32, "BONa")
        pg = rot([P.ps([128, 512], F32, f"pg{i}") for i in range(6)])
        pHd = [P.ps([128, 512], F32, f"pH{d}") for d in range(2)]
        def t64(name, dt=F32, w=TL, n=2):
            return rot([P.sb([64, w], dt, f"{name}{i}") for i in range(n)])
        def t128(name, w, dt=BF16, n=2):
            return rot([P.sb([128, w], dt, f"{name}{i}") for i in range(n)])
        tA = t64("tA"); tB = t64("tB"); tC = t64("tC"); tD = t64("tD"); tE = t64("tE"); tF = t64("tF"); tG = t64("tG", n=1); tH = t64("tH", n=1)
        tI = t64("tI", n=1); tJ = t64("tJ"); tK = t64("tK"); tL_ = t64("tL")
        gst = t64("gst")
        rrp = t64("rrt", n=1); kcp = t64("kct", n=1); kkp = t64("kkt", n=1); vvp = t64("vvt", n=1)
        bA = t64("bA", BF16); bR = t64("bR", BF16); bB = t64("bB", BF16); bK = t64("bK", BF16); bBh = t64("bBh", BF16); bKh = t64("bKh", BF16)
        pcs = rot([P.sb([64, 4], F32, f"pc{i}") for i in range(2)])
        ARp = rot([P.sb([64, 4, 256], BF16, f"AR{i}") for i in range(2)])
        Xp = t128("Xp", 128, BF16, n=8); XXp = t128("XXp", 256, BF16, n=16); LTRBp = t128("LTRB", 256, n=8); AKRKp = t128("AKRK", 256, n=8); TMp = t128("TM", 256, n=8)
        LTfp = t128("LTf", 128, BF16, n=8); L21p = t128("L21", 128, BF16, n=8); Z1p = t128("Z1", 128, BF16, n=8); Z2p = t128("Z2", 128, BF16, n=8)
        TTp = t128("TT", 128, BF16, n=16); AKVp = t128("AKV", 64, n=8); WUp = t128("WU", 128, n=8)
        QeTp = t64("QeT", BF16, 128, n=16); MTp = t64("MT", F32, 64, n=16); Gp = t64("G", F32, 64, n=16); QPs = t128("QPs", 128, BF16, n=4)
        HPpd = [rot([P.sb([64, 128], F32, f"HP{d}_{i}") for i in range(2)]) for d in range(2)]
        HPbpd = [rot([P.sb([64, 128], BF16, f"HPb{d}_{i}") for i in range(2)]) for d in range(2)]

        for h in range(6):
            pr = slice(64 * (h % 2), 64 * (h % 2) + 64)
            hc = slice(h * 64, (h + 1) * 64)
            for j in range(nt):
                ts_ = slice(j * TL, (j + 1) * TL)
                o3 = (tI(), tJ(), tK())
                for w3 in range(3):
                    rw = rawp[w3]()
                    P.dma("sp", rw[:], PT[2 + 3 * w3 + h // 2, pr, HL - 1 + j * TL:HL + (j + 1) * TL + 1], reads=[PT], writes=[rw])
                    o_ = o3[w3]
                    P.o("dve", "tensor_scalar", [rw, convw], [o_], out=o_[:], in0=rw[:, 0:TL], scalar1=convw[:, h, w3, 0:1], scalar2=0.0, op0=ALU.mult, op1=ALU.add)
                    P.o("dve", "scalar_tensor_tensor", [rw, convw, o_], [o_], out=o_[:], in0=rw[:, 1:TL + 1], scalar=convw[:, h, w3, 1:2], in1=o_[:], op0=ALU.mult, op1=ALU.add)
                    P.o("dve", "scalar_tensor_tensor", [rw, convw, o_], [o_], out=o_[:], in0=rw[:, 2:TL + 2], scalar=convw[:, h, w3, 2:3], in1=o_[:], op0=ALU.mult, op1=ALU.add)
                r_t, kc_t, v_t = o3
                kk_t = tL_()
                P.o("dve", "tensor_scalar", [kc_t, hp], [kk_t], out=kk_t[:], in0=kc_t[:], scalar1=hp[:, h, 0:1], scalar2=0.0, op0=ALU.mult, op1=ALU.add)
                sq = tA()
                P.o("pool", "tensor_tensor", [kk_t], [sq], out=sq[:], in0=kk_t[:], in1=kk_t[:], op=ALU.mult)
                p = pg()
                P.o("pe", "matmul", [ones64, sq], [p], p[0:64, :], lhsT=ones64[:], rhs=sq[:], start=True, stop=True)
                nr = tB()
                P.o("act", "activation", [p], [nr], out=nr[:], in_=p[0:64, :], func=AF.Sqrt)
                P.o("dve", "tensor_scalar", [nr], [nr], out=nr[:], in0=nr[:], scalar1=1e-12, scalar2=0.0, op0=ALU.max, op1=ALU.add)
                P.o("dve", "reciprocal", [nr], [nr], out=nr[:], in_=nr[:])
                P.o("dve", "tensor_tensor", [kk_t, nr], [kk_t], out=kk_t[:], in0=kk_t[:], in1=nr[:], op=ALU.mult)
                for qi, tt_ in enumerate((r_t, kc_t, kk_t, v_t)):
                    P.dma("sp", RKV[qi, :, ts_], tt_[:], reads=[tt_], writes=[RKV])
                p = pg()
                P.o("pe", "matmul", [gup, sgl], [p], p[0:64, :], lhsT=gup[:, hc], rhs=sgl[:, ts_], start=True, stop=True)
                g_ = gst()
                P.o("act", "copy", [p], [g_], out=g_[:], in_=p[0:64, :])
                P.dma("sp", GT[h, :, ts_], g_[:], reads=[g_], writes=[GT])
            P.o("pool", "memset", [], [YSa], YSa[:], 0.0)
            P.o("pool", "memset", [], [BONa], BONa[:], 0.0)
            HPd = {}; HPbd = {}
            for d in range(2):
                HP = HPpd[d](); HPb = HPbpd[d]()
                P.o("pool", "memset", [], [HP], HP[:, 0:64], 0.0)
                P.o("pool", "tensor_copy", [identf], [HP], out=HP[:, 64:128], in_=identf[0:64, 0:64])
                P.o("pool", "tensor_copy", [HP], [HPb], out=HPb[:], in_=HP[:])
                HPd[d] = HP; HPbd[d] = HPb
            def tile_work(d, j):
                if True:
                    ts_ = slice(j * TL, (j + 1) * TL)
                    rr = rrp(); kc = kcp(); kk = kkp(); vv = vvp()
                    for qi, tt_ in enumerate((rr, kc, kk, vv)):
                        P.dma("sp", tt_[:], RKV[qi, :, ts_], reads=[RKV], writes=[tt_])
                    vbf = vbfp()
                    P.o("pool", "tensor_copy", [vv], [vbf], out=vbf[:], in_=vv[:])
                    p = pg()
                    P.o("pe", "matmul", [dup, tanhw], [p], p[0:64, :], lhsT=dup[:, d, hc], rhs=tanhw[:, ts_], start=True, stop=True)
                    sg = tA()
                    P.o("act", "activation", [p, dirp], [sg], out=sg[:], in_=p[0:64, :], func=AF.Sigmoid, bias=dirp[:, h, d, 0:1], scale=1.0)
                    p = pg()
                    P.o("pe", "matmul", [iup, alat], [p], p[0:64, :], lhsT=iup[:, d, hc], rhs=alat[:, ts_], start=True, stop=True)
                    aa = tB()
                    P.o("act", "activation", [p, dirp], [aa], out=aa[:], in_=p[0:64, :], func=AF.Sigmoid, bias=dirp[:, h, d, 1:2], scale=1.0)
                    t1 = tC()
                    P.o("dve", "tensor_scalar", [aa, hp, omka], [t1], out=t1[:], in0=aa[:], scalar1=hp[:, h, 1:2], scalar2=omka[:, h:h + 1], op0=ALU.mult, op1=ALU.add)
                    keff = tD()
                    P.o("dve", "tensor_tensor", [kc, t1], [keff], out=keff[:], in0=kc[:], in1=t1[:], op=ALU.mult)
                    bb = tE()
                    P.o("pool", "tensor_tensor", [kk, aa], [bb], out=bb[:], in0=kk[:], in1=aa[:], op=ALU.mult)
                    tb = tC()
                    P.o("dve", "scalar_tensor_tensor", [rr, hp, keff], [tb], out=tb[:], in0=rr[:], scalar=hp[:, h, 2:3], in1=keff[:], op0=ALU.mult, op1=ALU.mult)
                    p = pg()
                    P.o("pe", "matmul", [ones64, tb], [p], p[0:64, :], lhsT=ones64[:], rhs=tb[:], start=True, stop=True)
                    t2 = tF()
                    P.o("dve", "tensor_tensor", [p, vv], [t2], out=t2[:], in0=p[0:64, :], in1=vv[:], op=ALU.mult)
                    P.o("pool", "tensor_tensor", [t2, BONa], [BONa], out=BONa[:, ts_], in0=t2[:], in1=BONa[:, ts_], op=ALU.add)
                    cs = tF()
                    P.o("dve", "tensor_tensor_scan", [mask0, sg], [cs], out=cs[:], data0=mask0[:], data1=sg[:], initial=0.0, op0=ALU.mult, op1=ALU.add)
                    csx = tG()
                    P.o("pool", "tensor_tensor", [cs, sg], [csx], out=csx[:], in0=cs[:], in1=sg[:], op=ALU.subtract)
                    rem = tH()
                    cs3 = cs[:].rearrange("p (c t) -> p c t", t=128)
                    P.o("dve", "tensor_tensor", [cs], [rem], out=rem[:].rearrange("p (c t) -> p c t", t=128), in0=cs3[:, :, 127:128].to_broadcast([64, 4, 128]), in1=cs3, op=ALU.subtract)
                    pc = pcs()
                    P.o("act", "activation", [cs], [pc], out=pc[:], in_=cs3[:, :, 127], func=AF.Exp, scale=-CDEC)
                    if d == 0:
                        Ein, Eex, Erem = cs, csx, rem
                    else:
                        remx = tI()
                        P.o("pool", "tensor_tensor", [rem, sg], [remx], out=remx[:], in0=rem[:], in1=sg[:], op=ALU.add)
                        Ein, Eex, Erem = remx, rem, csx
                    Pin = tJ(); Pex = tK(); Pinv = tL_(); Prem = tC()
                    P.o("act", "activation", [Ein], [Pin], out=Pin[:], in_=Ein[:], func=AF.Exp, scale=-CDEC)
                    P.o("act", "activation", [Eex], [Pex], out=Pex[:], in_=Eex[:], func=AF.Exp, scale=-CDEC)
                    P.o("act", "activation", [Ein], [Pinv], out=Pinv[:], in_=Ein[:], func=AF.Exp, scale=CDEC)
                    P.o("act", "activation", [Erem], [Prem], out=Prem[:], in_=Erem[:], func=AF.Exp, scale=-CDEC)
                    AR = ARp(); bT = bB(); kT = bK(); bh = bBh(); kh = bKh()
                    v3 = lambda ap: ap.rearrange("p (c t) -> p c t", t=128)
                    P.o("dve", "scalar_tensor_tensor", [kk, Pex], [AR], out=AR[:, :, 0:128], in0=v3(kk[:]), scalar=-1.0, in1=v3(Pex[:]), op0=ALU.mult, op1=ALU.mult)
                    P.o("pool", "tensor_tensor", [rr, Pin], [AR], out=AR[:, :, 128:256], in0=v3(rr[:]), in1=v3(Pin[:]), op=ALU.mult)
                    P.o("dve", "tensor_tensor", [bb, Pinv], [bT], out=bT[:], in0=bb[:], in1=Pinv[:], op=ALU.mult)
                    P.o("pool", "tensor_tensor", [keff, Pinv], [kT], out=kT[:], in0=keff[:], in1=Pinv[:], op=ALU.mult)
                    P.o("dve", "tensor_tensor", [bb, Prem], [bh], out=bh[:], in0=bb[:], in1=Prem[:], op=ALU.mult)
                    P.o("pool", "tensor_tensor", [keff, Prem], [kh], out=kh[:], in0=keff[:], in1=Prem[:], op=ALU.mult)
                    chunks = list(range(4)) if d == 0 else list(range(3, -1, -1))
                    def local_gen(c, res):
                        sl = slice(c * 128, (c + 1) * 128)
                        tok = slice(j * TL + c * 128, j * TL + (c + 1) * 128)
                        yield
                        p = pg()
                        P.o("pe", "matmul", [AR, bT], [p], p[:, 0:128], lhsT=AR[:, c, 0:128], rhs=bT[:, sl], start=True, stop=True)
                        X = Xp()
                        P.o("dve", "tensor_tensor", [p, maskT], [X], out=X[:], in0=p[:, 0:128], in1=maskT[:, d, 256:384], op=ALU.mult)
                        yield
                        p = pg()
                        P.o("pe", "matmul", [bT, AR], [p], p[:, 0:256], lhsT=bT[:, sl], rhs=AR[:, c, :], start=True, stop=True)
                        LTRB = LTRBp()
                        P.o("dve", "tensor_tensor", [p, maskT], [LTRB], out=LTRB[:, 128:256], in0=p[:, 128:256], in1=maskT[:, d, 128:256], op=ALU.mult)
                        LTf = LTfp()
                        P.o("dve", "tensor_tensor", [p, maskT], [LTf], out=LTf[:], in0=p[:, 0:128], in1=maskT[:, d, 384:512], op=ALU.mult)
                        L21T = L21p()
                        P.o("dve", "tensor_tensor", [p, maskT], [L21T], out=L21T[:], in0=p[:, 0:128], in1=maskT[:, d, 512:640], op=ALU.mult)
                        yield
                        p = pg()
                        P.o("pe", "matmul", [kT, AR], [p], p[:, 0:256], lhsT=kT[:, sl], rhs=AR[:, c, :], start=True, stop=True)
                        AKRK = AKRKp()
                        P.o("dve", "tensor_tensor", [p, maskT], [AKRK], out=AKRK[:], in0=p[:, 0:256], in1=maskT[:, d, 0:256], op=ALU.mult)
                        yield
                        p = pg()
                        P.o("pe", "matmul", [AR, identb], [p], p[:, 0:64], lhsT=AR[:, c, 0:128], rhs=identb[0:64, 0:64], start=True, stop=True)
                        for qi, src in ((1, bh), (2, kh)):
                            P.o("pe", "matmul", [src, identb], [p], p[:, qi * 64:(qi + 1) * 64], lhsT=src[:, sl], rhs=identb[0:64, 0:64], start=True, stop=True)
                        P.o("pe", "matmul", [vbf, identb], [p], p[:, 192:256], lhsT=vbf[:, sl], rhs=identb[0:64, 0:64], start=True, stop=True)
                        TM = TMp()
                        P.o("act", "copy", [p], [TM], out=TM[:], in_=p[:, 0:256])
                        yield
                        TT = TTp()
                        P.o("pool", "tensor_tensor", [identb, LTf], [TT], out=TT[:], in0=identb[:], in1=LTf[:], op=ALU.add)
                        Xc = X[:]; XTc = LTf[:]; Xb = X; XTb = LTf
                        for i in range(5):
                            p = pg()
                            P.o("pe", "matmul", [Xb, XTb], [p], p[:, 0:128], lhsT=XTc, rhs=Xc, start=True, stop=True)
                            if i < 4:
                                P.o("pe", "matmul", [Xb, XTb], [p], p[:, 128:256], lhsT=Xc, rhs=XTc, start=True, stop=True)
                            XX = XXp()
                            if i < 4:
                                P.o("act", "copy", [p], [XX], out=XX[:], in_=p[:, 0:256])
                            else:
                                P.o("act", "copy", [p], [XX], out=XX[:, 0:128], in_=p[:, 0:128])
                            Xc = XX[:, 0:128]; XTc = XX[:, 128:256]; Xb = XX; XTb = XX
                            yield
                            p2 = pg()
                            P.o("pe", "matmul", [XX, TT], [p2], p2[:, 0:128], lhsT=Xc, rhs=TT[:], start=True, stop=True)
                            TTn = TTp()
                            P.o("dve", "tensor_tensor", [p2, TT], [TTn], out=TTn[:], in0=p2[:, 0:128], in1=TT[:], op=ALU.add)
                            TT = TTn
                            yield
                        yield
                        p = pg()
                        P.o("pe", "matmul", [AKRK, TM], [p], p[:, 0:64], lhsT=AKRK[:, 0:128], rhs=TM[:, 192:256], start=True, stop=True)
                        AKV = AKVp()
                        P.o("act", "copy", [p], [AKV], out=AKV[:], in_=p[:, 0:64])
                        yield
                        p = pg()
                        P.o("pe", "matmul", [TT, TM], [p], p[:, 0:64], lhsT=TT[:], rhs=TM[:, 0:64], start=True, stop=True)
                        P.o("pe", "matmul", [TT, AKV], [p], p[:, 64:128], lhsT=TT[:], rhs=AKV[:], start=True, stop=True)
                        Z1 = Z1p()
                        P.o("act", "copy", [p], [Z1], out=Z1[:], in_=p[:, 0:128])
                        yield
                        p = pg()
                        P.o("pe", "matmul", [L21T, Z1], [p], p[:, 0:128], lhsT=L21T[:], rhs=Z1[:], start=True, stop=True)
                        Z2 = Z2p()
                        P.o("act", "copy", [p], [Z2], out=Z2[:], in_=p[:, 0:128])
                        yield
                        p = pg()
                        P.o("pe", "matmul", [TT, Z2], [p], p[:, 0:128], lhsT=TT[:], rhs=Z2[:], start=True, stop=True)
                        WU = WUp()
                        P.o("dve", "tensor_tensor", [p, Z1], [WU], out=WU[:], in0=p[:, 0:128], in1=Z1[:], op=ALU.add)
                        yield
                        p = pg()
                        P.o("pe", "matmul", [WU, LTRB], [p], p[0:64, 0:128], lhsT=WU[:, 0:64], rhs=LTRB[:, 128:256], start=True, stop=True)
                        QeT = QeTp()
                        P.o("dve", "tensor_tensor", [p, AR], [QeT], out=QeT[:], in0=p[0:64, 0:128], in1=AR[:, c, 128:256], op=ALU.add)
                        yield
                        p = pg()
                        P.o("pe", "matmul", [WU, LTRB], [p], p[0:64, 0:128], lhsT=WU[:, 64:128], rhs=LTRB[:, 128:256], start=True, stop=False)
                        P.o("pe", "matmul", [TM, AKRK], [p], p[0:64, 0:128], lhsT=TM[:, 192:256], rhs=AKRK[:, 128:256], start=False, stop=True)
                        P.o("dve", "tensor_tensor", [p, YSa], [YSa], out=YSa[:, tok], in0=p[0:64, 0:128], in1=YSa[:, tok], op=ALU.add)
                        yield
                        p = pg()
                        P.o("pe", "matmul", [WU, TM], [p], p[0:64, 0:64], lhsT=WU[:, 0:64], rhs=TM[:, 64:128], start=True, stop=True)
                        MT = MTp()
                        P.o("dve", "scalar_tensor_tensor", [identf, pc, p], [MT], out=MT[:], in0=identf[0:64, 0:64], scalar=pc[:, c:c + 1], in1=p[0:64, 0:64], op0=ALU.mult, op1=ALU.add)
                        yield
                        p = pg()
                        P.o("pe", "matmul", [TM, WU], [p], p[0:64, 0:64], lhsT=TM[:, 64:128], rhs=WU[:, 64:128], start=True, stop=False)
                        P.o("pe", "matmul", [TM], [p], p[0:64, 0:64], lhsT=TM[:, 128:192], rhs=TM[:, 192:256], start=False, stop=True)
                        G = Gp()
                        P.o("act", "copy", [p], [G], out=G[:], in_=p[0:64, 0:64])
                        res.update(QeT=QeT, MT=MT, G=G, tok=tok)
                        yield
                    results = {c: {} for c in chunks}
                    gens = [local_gen(c, results[c]) for c in chunks]
                    return chunks, results, gens
            def chain_gen(d, chunks, results):
                HP = HPd[d]; HPb = HPbd[d]
                for c in chunks:
                    QeT = results[c]['QeT']; MT = results[c]['MT']; G = results[c]['G']; tok = results[c]['tok']
                    p = pg()
                    P.o("pe", "matmul", [HPb, QeT], [p], p[:, 0:128], lhsT=HPb[:], rhs=QeT[:], start=True, stop=True)
                    P.o("dve", "tensor_tensor", [p, YSa], [YSa], out=YSa[:, tok], in0=p[0:64, 0:128], in1=YSa[:, tok], op=ALU.add)
                    qps = QPs()
                    P.o("act", "copy", [p], [qps], out=qps[64:128, :], in_=p[64:128, 0:128])
                    P.dma("sp", QP[d, h, :, tok], qps[64:128, :], reads=[qps], writes=[QP])
                    yield
                    pH_ = pHd[d]
                    P.o("pe", "matmul", [MT, HP], [pH_], pH_[0:64, 0:128], lhsT=MT[:], rhs=HP[:], start=True, stop=True)
                    HPn = HPpd[d](); HPbn = HPbpd[d]()
                    P.o("dve", "tensor_tensor", [pH_, G], [HPn], out=HPn[:, 0:64], in0=pH_[0:64, 0:64], in1=G[:], op=ALU.add)
                    P.o("act", "copy", [pH_], [HPn], out=HPn[:, 64:128], in_=pH_[0:64, 64:128])
                    yield
                    P.o("pool", "tensor_copy", [HPn], [HPbn], out=HPbn[:], in_=HPn[:])
                    HP = HPn; HPb = HPbn
                    HPd[d] = HP; HPbd[d] = HPb
                    yield

            def run_lockstep(gl):
                alive = list(gl)
                while alive:
                    nxt_alive = []
                    for g_ in alive:
                        try:
                            next(g_)
                            nxt_alive.append(g_)
                        except StopIteration:
                            pass
                    alive = nxt_alive

            carry = []
            for i in range(nt):
                pend = []
                for d in range(2):
                    j = i if d == 0 else nt - 1 - i
                    chunks, results, gens = tile_work(d, j)
                    pend.append((d, chunks, results, gens))
                run_lockstep(carry + [g_ for (_, _, _, gl) in pend for g_ in gl])
                carry = [chain_gen(d, chunks, results) for (d, chunks, results, _) in pend]
            run_lockstep(carry)
            for d in range(2):
                P.dma("sp", HEND[d, h, :, :], HPd[d][:], reads=[HPd[d]], writes=[HEND])
            P.dma("sp", YS[h, :, :], YSa[:], reads=[YSa], writes=[YS])
            P.dma("sp", BON[h, :, :], BONa[:], reads=[BONa], writes=[BON])
        P.phase_barrier()
    P.es = es_outer


T2 = 256
GN_EPS = 64e-5


def phase_fin(P, es_outer, YS, QP, BON, GT, PRED, hp_d, YR, NTl):
    TL = 512
    with ExitStack() as es:
        P.es = es
        hp = P.sb([64, 6, 5], F32, "hp4")
        P.dma("sp", hp[:], hp_d[:], reads=[hp_d], writes=[hp])
        o64 = P.sb([64, 64], F32, "o64")
        P.o("pool", "memset", [], [o64], o64[:], 1.0 / 64)
        epsg = P.sb([64, 1], F32, "epsg")
        P.o("pool", "memset", [], [epsg], epsg[:], GN_EPS)
        pg = rot([P.ps([128, 512], F32, f"fg{i}") for i in range(6)])
        pred = rot([P.sb([64, 128], F32, f"pred{i}") for i in range(3)])
        Sb = rot([P.sb([64, 64], F32, f"Sf{i}") for i in range(3)])
        Hb = [P.sb([64, 64], BF16, f"Hb{d}") for d in range(2)]
        def t64(name, dt=F32, n=2):
            return rot([P.sb([64, TL], dt, f"{name}{i}") for i in range(n)])
        ys_t = t64("ys"); qp_t = [t64("qp0", BF16), t64("qp1", BF16)]; bon_t = t64("bon"); g_t = t64("g")
        y_t = t64("y"); yc_t = t64("yc"); sq_t = t64("sq"); rs_t = t64("rs"); out_t = t64("out")
        for h in range(6):
            for d in range(2):
                S = Sb()
                P.o("pool", "memset", [], [S], S[:], 0.0)
                for slot in range(3):
                    pd = pred()
                    P.dma("sp", pd[:], PRED[d, h, slot, :, :], reads=[PRED], writes=[pd])
                    p = pg()
                    P.o("pe", "matmul", [pd, S], [p], p[0:64, 0:64], lhsT=pd[:, 0:64], rhs=S[:], start=True, stop=True)
                    Sn = Sb()
                    P.o("dve", "tensor_tensor", [p, pd], [Sn], out=Sn[:], in0=p[0:64, 0:64], in1=pd[:, 64:128], op=ALU.add)
                    S = Sn
                P.o("act", "copy", [S], [Hb[d]], out=Hb[d][:], in_=S[:])
            for j in range(NTl // TL):
                ts_ = slice(j * TL, (j + 1) * TL)
                ys = ys_t(); bon = bon_t(); g = g_t()
                P.dma("sp", ys[:], YS[h, :, ts_], reads=[YS], writes=[ys])
                P.dma("sp", bon[:], BON[h, :, ts_], reads=[BON], writes=[bon])
                P.dma("sp", g[:], GT[h, :, ts_], reads=[GT], writes=[g])
                p = pg()
                for d in range(2):
                    q = qp_t[d]()
                    P.dma("sp", q[:], QP[d, h, :, ts_], reads=[QP], writes=[q])
                    P.o("pe", "matmul", [Hb[d], q], [p], p[0:64, :], lhsT=Hb[d][:], rhs=q[:], start=(d == 0), stop=(d == 1))
                y = y_t()
                P.o("dve", "tensor_tensor", [p, ys], [y], out=y[:], in0=p[0:64, :], in1=ys[:], op=ALU.add)
                p = pg()
                P.o("pe", "matmul", [o64, y], [p], p[0:64, :], lhsT=o64[:], rhs=y[:], start=True, stop=True)
                yc = yc_t()
                P.o("dve", "tensor_tensor", [y, p], [yc], out=yc[:], in0=y[:], in1=p[0:64, :], op=ALU.subtract)
                sq = sq_t()
                P.o("pool", "tensor_tensor", [yc], [sq], out=sq[:], in0=yc[:], in1=yc[:], op=ALU.mult)
                p = pg()
                P.o("pe", "matmul", [o64, sq], [p], p[0:64, :], lhsT=o64[:], rhs=sq[:], start=True, stop=True)
                rs = rs_t()
                P.o("act", "activation", [p, epsg], [rs], out=rs[:], in_=p[0:64, :], func=AF.Sqrt, bias=epsg[:], scale=1.0)
                P.o("dve", "reciprocal", [rs], [rs], out=rs[:], in_=rs[:])
                P.o("dve", "tensor_tensor", [yc, rs], [yc], out=yc[:], in0=yc[:], in1=rs[:], op=ALU.mult)
                P.o("dve", "tensor_scalar", [yc, hp], [yc], out=yc[:], in0=yc[:], scalar1=hp[:, h, 3:4], scalar2=hp[:, h, 4:5], op0=ALU.mult, op1=ALU.add)
                P.o("pool", "tensor_tensor", [yc, bon], [yc], out=yc[:], in0=yc[:], in1=bon[:], op=ALU.add)
                o_ = out_t()
                P.o("dve", "tensor_tensor", [yc, g], [o_], out=o_[:], in0=yc[:], in1=g[:], op=ALU.mult)
                P.dma("sp", YR[h, :, ts_], o_[:], reads=[o_], writes=[YR])
        P.phase_barrier()
    P.es = es_outer


def phase_E(P, es_outer, xT, YP, YR, YA, wout_d, gpost_d, gffn_d, gpost2_d, wfi_d, wfo_d, XO, NTl, xoff):
    nt = NTl // T2
    with ExitStack() as es:
        P.es = es
        Wo = P.sb([128, 8, 1024], BF16, "Wo")
        Wi = P.sb([128, 8, 4096], BF16, "Wi"); Wo2 = P.sb([128, 32, 1024], BF16, "Wo2")
        for c in range(8):
            P.dma("pool", Wo[:, c, :], wout_d[c * 128:(c + 1) * 128, :], reads=[wout_d], writes=[Wo])
        for c in range(8):
            P.dma("pool", Wi[:, c, :], wfi_d[c, :, :], reads=[wfi_d], writes=[Wi])
        for c in range(32):
            P.dma("pool", Wo2[:, c, :], wfo_d[c, :, :], reads=[wfo_d], writes=[Wo2])
        g1 = P.sb([128, 8], F32, "g1"); g2 = P.sb([128, 8], F32, "g2"); g3 = P.sb([128, 8], F32, "g3")
        P.dma("sp", g1[:], gpost_d[:], reads=[gpost_d], writes=[g1])
        P.dma("sp", g2[:], gffn_d[:], reads=[gffn_d], writes=[g2])
        P.dma("sp", g3[:], gpost2_d[:], reads=[gpost2_d], writes=[g3])
        ones_bf = P.sb([128, 128], BF16, "ones4"); eps_t = P.sb([128, 1], F32, "eps4")
        P.o("dve", "memset", [], [ones_bf], ones_bf[:], 1.0)
        P.o("dve", "memset", [], [eps_t], eps_t[:], EPS)
        xs2 = [P.sb([128, 8, T2], F32, f"x4_{i}") for i in range(2)]
        ym2 = [P.sb([128, 8, T2], BF16, f"ym_{i}") for i in range(2)]
        mx = P.sb([128, 8, T2], F32, "mx")
        sq = P.sb([128, 8, T2], BF16, "sq4"); hT = P.sb([128, 8, T2], BF16, "hT4")
        aT = P.sb([128, 32, T2], BF16, "aT")
        tmp = P.sb([128, T2], F32, "tmp4"); rstd = P.sb([128, T2], F32, "rstd4")
        rl = rot([P.sb([128, T2], F32, f"rl{i}") for i in range(3)])
        ss_ps = P.ps([128, 512], F32, "ss4")
        mm = rot([P.ps([128, 512], F32, f"m4_{i}") for i in range(5)])
        def load(j):
            ts_ = slice(j * T2, (j + 1) * T2)
            b = j % 2
            P.dma("sp", xs2[b][:], xT[:, :, xoff + j * T2:xoff + (j + 1) * T2].rearrange("c p t -> p c t"), reads=[xT], writes=[xs2[b]])
            P.dma("pool", ym2[b][:, 0:2, :], YP[:, :, ts_].rearrange("c p t -> p c t"), reads=[YP], writes=[ym2[b]])
            P.dma("pool", ym2[b][:, 2:5, :], YR[:, :, ts_].rearrange("(c h) p t -> (h p) c t", h=2), reads=[YR], writes=[ym2[b]])
            P.dma("pool", ym2[b][:, 5:8, :], YA[:, :, ts_].rearrange("(c h) p t -> (h p) c t", h=2), reads=[YA], writes=[ym2[b]])
        def rms(src):
            P.o("act", "activation", [src], [sq], out=sq[:], in_=src[:], func=AF.Square)
            for c in range(8):
                P.o("pe", "matmul", [ones_bf, sq], [ss_ps], ss_ps[:, 0:T2], lhsT=ones_bf[:], rhs=sq[:, c, :], start=(c == 0), stop=(c == 7))
            P.o("act", "activation", [ss_ps, eps_t], [tmp], out=tmp[:], in_=ss_ps[:, 0:T2], func=AF.Sqrt, scale=1.0 / D, bias=eps_t[:])
            P.o("dve", "reciprocal", [tmp], [rstd], out=rstd[:], in_=tmp[:])
        load(0)
        ev = 0
        for j in range(nt):
            if j + 1 < nt:
                load(j + 1)
            b = j % 2
            xs = xs2[b]; ym = ym2[b]
            for m in range(8):
                ps = mm(); ms = slice(m * 128, (m + 1) * 128)
                for c in range(8):
                    P.o("pe", "matmul", [Wo, ym], [ps], ps[:, 0:T2], lhsT=Wo[:, c, ms], rhs=ym[:, c, :], start=(c == 0), stop=(c == 7))
                if m % 2:
                    P.o("act", "copy", [ps], [mx], out=mx[:, m, :], in_=ps[:, 0:T2])
                else:
                    P.o("dve", "tensor_copy", [ps], [mx], out=mx[:, m, :], in_=ps[:, 0:T2])
            rms(mx)
            for c in range(8):
                P.o("dve", "scalar_tensor_tensor", [mx, g1, rstd], [mx], out=mx[:, c, :], in0=mx[:, c, :], scalar=g1[:, c:c + 1], in1=rstd[:], op0=ALU.mult, op1=ALU.mult)
            P.o("pool", "tensor_tensor", [xs, mx], [xs], out=xs[:], in0=xs[:], in1=mx[:], op=ALU.add)
            rms(xs)
            for c in range(8):
                P.o("dve", "scalar_tensor_tensor", [xs, g2, rstd], [hT], out=hT[:, c, :], in0=xs[:, c, :], scalar=g2[:, c:c + 1], in1=rstd[:], op0=ALU.mult, op1=ALU.mult)
            for f in range(32):
                ps = mm()
                for c in range(8):
                    P.o("pe", "matmul", [Wi, hT], [ps], ps[:, 0:T2], lhsT=Wi[:, c, f * 128:(f + 1) * 128], rhs=hT[:, c, :], start=(c == 0), stop=(c == 7))
                r_ = rl()
                P.o("act", "activation", [ps], [r_], out=r_[:], in_=ps[:, 0:T2], func=AF.Relu)
                P.o("dve" if f % 2 else "pool", "tensor_tensor", [r_], [aT], out=aT[:, f, :], in0=r_[:], in1=r_[:], op=ALU.mult)
            for m in range(8):
                ps = mm(); ms = slice(m * 128, (m + 1) * 128)
                for f in range(32):
                    P.o("pe", "matmul", [Wo2, aT], [ps], ps[:, 0:T2], lhsT=Wo2[:, f, ms], rhs=aT[:, f, :], start=(f == 0), stop=(f == 31))
                if m % 2:
                    P.o("act", "copy", [ps], [mx], out=mx[:, m, :], in_=ps[:, 0:T2])
                else:
                    P.o("dve", "tensor_copy", [ps], [mx], out=mx[:, m, :], in_=ps[:, 0:T2])
            rms(mx)
            for c in range(8):
                P.o("dve", "scalar_tensor_tensor", [mx, g3, rstd], [mx], out=mx[:, c, :], in0=mx[:, c, :], scalar=g3[:, c:c + 1], in1=rstd[:], op0=ALU.mult, op1=ALU.mult)
            P.o("pool", "tensor_tensor", [xs, mx], [xs], out=xs[:], in0=xs[:], in1=mx[:], op=ALU.add)
            P.dma("sp", XO[:, :, j * T2:(j + 1) * T2].rearrange("c p t -> p c t"), xs[:], reads=[xs], writes=[XO])
        P.phase_barrier()
    P.es = es_outer

NEGV = -1.0e30

def t5_bucket_np(rel):
    nb = 16; max_exact = 8
    ret = (rel > 0).astype(np.int32) * nb
    n = np.abs(rel)
    nf = np.maximum(n, max_exact).astype(np.float32)
    large = max_exact + (np.log(nf / np.float32(max_exact)) / np.float32(np.log(1024 / max_exact)) * np.float32(nb - max_exact)).astype(np.int32)
    large = np.minimum(large, nb - 1)
    return ret + np.where(n < max_exact, n, large)

def att_bias_tiles(rel_bias):
    out = np.full((128, 36, 128), NEGV, np.float32)
    j = np.arange(128)[:, None]; i = np.arange(128)[None, :]
    for br, dl in enumerate((1, 4, 16)):
        for kt in range(2):
            rel = (128 * kt + j - 64) - i
            valid = np.abs(rel) <= 64
            bk = t5_bucket_np(rel * dl)
            for h in range(6):
                vals = rel_bias[bk, h]
                out[:, (br * 6 + h) * 2 + kt, :] = np.where(valid, vals, np.float32(NEGV))
    return out

def pool_consts(pool_w, pool_scale, first, last, NTl):
    bd = np.zeros((2, 128, 128), np.float32)
    for g in range(4):
        c, hf = divmod(g, 2)
        bd[c, 64 * hf:64 * hf + 64, 64 * hf:64 * hf + 64] = pool_w[g]
    psc = np.ascontiguousarray(pool_scale.reshape(2, 128).T)
    wins = (2, 4, 8, 16)
    invw = np.zeros((128, 2), np.float32); edge = np.ones((128, 2, 16), np.float32)
    for g in range(4):
        c, hf = divmod(g, 2); W = wins[g]
        invw[64 * hf:64 * hf + 64, c] = 1.0 / W
        for t in range(8):
            if first:
                cnt = min(t + W - W // 2, 10 ** 9) - max(t - W // 2, 0)
                edge[64 * hf:64 * hf + 64, c, t] = W / cnt
            if last:
                tt = NTl - 8 + t
                cnt = min(tt + W - W // 2, NTl) - (tt - W // 2)
                edge[64 * hf:64 * hf + 64, c, 8 + t] = W / cnt
    return bd, psc, invw, edge

def kmask_tiles(valid_ext, NTl, HL=1024):
    SB = 2048
    cols = []
    for sb in range(NTl // SB):
        t0 = sb * SB
        for br, dl in enumerate((1, 4, 16)):
            nb = SB // (128 * dl)
            for r in range(dl):
                for b in range(nb):
                    for kt in range(2):
                        e0 = t0 + HL + r + dl * (128 * b - 64 + 128 * kt)
                        idx = e0 + dl * np.arange(128)
                        cols.append(np.where(valid_ext[idx], 0.0, NEGV).astype(np.float32))
    return np.ascontiguousarray(np.stack(cols, axis=1))

def rwkv_consts(rwkv_conv, key_k, key_a, bonus_rk, gn_gain, gn_bias, decay_w0, iclr_a0):
    convw = np.zeros((64, 6, 3, 3), np.float32)
    for h in range(6):
        for w3 in range(3):
            ch = w3 * 384 + h * 64
            convw[:, h, w3, :] = rwkv_conv[:, ch:ch + 64].T
    hp = np.stack([a.reshape(6, 64).T for a in (key_k, key_a, bonus_rk, gn_gain, gn_bias)], axis=-1).astype(np.float32)
    dirp = np.zeros((64, 6, 2, 2), np.float32)
    for d in range(2):
        dirp[:, :, d, 0] = decay_w0[d].reshape(6, 64).T
        dirp[:, :, d, 1] = iclr_a0[d].reshape(6, 64).T
    t = np.arange(128)[:, None]; s = np.arange(128)[None, :]
    maskA = np.zeros((128, 2, 128), np.float32)
    maskA[:, 0, :] = (s < t); maskA[:, 1, :] = (s > t)
    maskT = np.zeros((128, 2, 256), np.float32)
    ss = np.arange(128)[:, None]; tt = np.arange(128)[None, :]
    maskT[:, 0, 0:128] = (ss < tt); maskT[:, 0, 128:256] = (ss <= tt)
    maskT[:, 1, 0:128] = (ss > tt); maskT[:, 1, 128:256] = (ss >= tt)
    mT = np.zeros((128, 2, 640), np.float32)
    mT[:, :, 0:256] = maskT
    for d in range(2):
        A = maskA[:, d, :]
        same = (t // 64) == (s // 64)
        Abd = A * same; A21 = A * (~same)
        mT[:, d, 256:384] = Abd
        mT[:, d, 384:512] = Abd.T
        mT[:, d, 512:640] = A21.T
    maskT = mT
    return np.ascontiguousarray(convw), np.ascontiguousarray(hp), dirp, maskA, maskT


from concourse.bass_utils import run_bass_kernel_spmd

LAYER_KEYS = ("norm_mix_pre", "norm_mix_post", "norm_ffn_pre", "norm_ffn_post", "w_in", "w_out", "pool_w", "pool_scale", "rwkv_conv",
              "decay_w0", "decay_up", "iclr_a0", "iclr_up", "gate_up", "key_k", "key_a", "bonus_rk", "gn_gain", "gn_bias", "w_ff_in", "w_ff_out")


def build_L2(NTl):
    NEl = NTl + 2 * HL
    nc = bass.Bass("TRN2", target_bir_lowering=False)
    with ExitStack() as es:
        P = Prog(nc, es)
        I_ = lambda n, s, dt=F32: P.dram(n, s, dt, kind="ExternalInput")
        O_ = lambda n, s, dt=F32: P.dram(n, s, dt, kind="ExternalOutput")
        xT = I_("xT", [8, 128, NEl]); gain = I_("gain", [128, 8]); w_in = I_("w_in", [8, 128, INW])
        poolw = I_("poolw", [2, 128, 128]); pscale = I_("pscale", [128, 2]); invw = I_("invw", [128, 2]); edge = I_("edge", [128, 2, 16])
        abias = I_("abias", [128, 36, 128]); kmask = I_("kmask", [128, (NTl // 2048) * 96])
        convw = I_("convw", [64, 6, 3, 3]); hp = I_("hp", [64, 6, 5]); dirp = I_("dirp", [64, 6, 2, 2])
        dup = I_("dup", [2, 64, 384]); iup = I_("iup", [2, 64, 384]); gup = I_("gup", [128, 384])
        maskA = I_("maskA", [128, 2, 128]); maskT = I_("maskT", [128, 2, 640])
        PT = P.dram("PT", [22, 128, NEl], F32)
        Vt = P.dram("Vt", [NEl, 384], BF16)
        YP = O_("YP", [2, 128, NTl]); YA = O_("YA", [6, 64, NTl])
        YS = O_("YS", [6, 64, NTl]); QP = O_("QP", [2, 6, 64, NTl], BF16)
        BON = O_("BON", [6, 64, NTl]); GT = O_("GT", [6, 64, NTl]); HEND = O_("HEND", [2, 6, 64, 128])
        phase_A(P, es, xT, gain, w_in, PT, Vt, NEl // TS)
        phase_pool(P, es, PT, poolw, pscale, invw, edge, YP, NTl)
        phase_att(P, es, PT, Vt, abias, kmask, YA, NTl)
        phase_rwkv(P, es, PT, convw, hp, dirp, dup, iup, gup, maskA, maskT, YS, QP, BON, GT, HEND, NTl)
        P.barrier_all_dma("sp")
        P.emit()
    return nc


def build_L3(NTl):
    nc = bass.Bass("TRN2", target_bir_lowering=False)
    with ExitStack() as es:
        P = Prog(nc, es)
        I_ = lambda n, s, dt=F32: P.dram(n, s, dt, kind="ExternalInput")
        O_ = lambda n, s, dt=F32: P.dram(n, s, dt, kind="ExternalOutput")
        xT = I_("xT", [8, 128, NTl])
        YP = I_("YP", [2, 128, NTl]); YA = I_("YA", [6, 64, NTl])
        YS = I_("YS", [6, 64, NTl]); QP = I_("QP", [2, 6, 64, NTl], BF16)
        BON = I_("BON", [6, 64, NTl]); GT = I_("GT", [6, 64, NTl]); PRED = I_("PRED", [2, 6, 3, 64, 128])
        hp = I_("hp", [64, 6, 5])
        wout = I_("wout", [1024, 1024]); gpost = I_("gpost", [128, 8]); gffn = I_("gffn", [128, 8]); gpost2 = I_("gpost2", [128, 8])
        wfi = I_("wfi", [8, 128, 4096]); wfo = I_("wfo", [32, 128, 1024])
        YR = O_("YR", [6, 64, NTl])
        XO = O_("XO", [8, 128, NTl])
        phase_fin(P, es, YS, QP, BON, GT, PRED, hp, YR, NTl)
        phase_E(P, es, xT, YP, YR, YA, wout, gpost, gffn, gpost2, wfi, wfo, XO, NTl, 0)
        P.barrier_all_dma("sp")
        P.emit()
    return nc


_PROGS = {}
_DBG = []


def _prog(kind, NTl):
    key = (kind, NTl)
    if key not in _PROGS:
        _PROGS[key] = build_L2(NTl) if kind == "L2" else build_L3(NTl)
    return _PROGS[key]


def g8(v):
    return np.ascontiguousarray(np.asarray(v, np.float32).reshape(8, 128).T)


def run_trunk(seqs, inputs, NTl):
    segs = []
    for si, x in enumerate(seqs):
        n = x.shape[0] // NTl
        for s in range(n):
            segs.append((si, s, n))
    ncores = len(segs)
    NEl = NTl + 2 * HL
    depth = inputs["w_in"].shape[0]
    X = [np.ascontiguousarray(seqs[si][s * NTl:(s + 1) * NTl].T.reshape(8, 128, NTl)) for (si, s, n) in segs]
    abias = att_bias_tiles(np.asarray(inputs["rel_bias"], np.float32))
    for l in range(depth):
        L = {k: np.asarray(inputs[k][l], np.float32) for k in LAYER_KEYS}
        convw, hp, dirp, maskA, maskT = rwkv_consts(L["rwkv_conv"], L["key_k"], L["key_a"], L["bonus_rk"], L["gn_gain"], L["gn_bias"], L["decay_w0"], L["iclr_a0"])
        in2 = []
        for ci, (si, s, n) in enumerate(segs):
            xe = np.zeros((8, 128, NEl), np.float32)
            xe[:, :, HL:HL + NTl] = X[ci]
            valid = np.zeros(NEl, bool); valid[HL:HL + NTl] = True
            if s > 0:
                xe[:, :, 0:HL] = X[ci - 1][:, :, NTl - HL:NTl]; valid[0:HL] = True
            if s < n - 1:
                xe[:, :, HL + NTl:] = X[ci + 1][:, :, 0:HL]; valid[HL + NTl:] = True
            bd, psc, iw, ed = pool_consts(L["pool_w"], L["pool_scale"], s == 0, s == n - 1, NTl)
            in2.append({"xT": xe, "gain": g8(L["norm_mix_pre"]), "w_in": np.ascontiguousarray(L["w_in"].reshape(8, 128, INW)),
                        "poolw": bd, "pscale": psc, "invw": iw, "edge": ed, "abias": abias, "kmask": kmask_tiles(valid, NTl),
                        "convw": convw, "hp": hp, "dirp": dirp, "dup": np.ascontiguousarray(L["decay_up"]), "iup": np.ascontiguousarray(L["iclr_up"]),
                        "gup": np.ascontiguousarray(L["gate_up"]), "maskA": maskA, "maskT": maskT})
        r2 = run_bass_kernel_spmd(_prog("L2", NTl), in2, core_ids=list(range(ncores))).results
        in3 = []
        for ci, (si, s, n) in enumerate(segs):
            pred = np.zeros((2, 6, 3, 64, 128), np.float32)
            fw_pred = [ci - s + j for j in range(s)]
            bw_pred = [ci - s + j for j in range(n - 1, s, -1)]
            for d, plist in ((0, fw_pred), (1, bw_pred)):
                off = 3 - len(plist)
                for k, cj in enumerate(plist):
                    he = np.asarray(r2[cj]["HEND"], np.float32)[d]
                    pred[d, :, off + k, :, 0:64] = np.transpose(he[:, :, 64:128], (0, 2, 1))
                    pred[d, :, off + k, :, 64:128] = he[:, :, 0:64]
            o = r2[ci]
            in3.append({"xT": X[ci], "YP": o["YP"], "YA": o["YA"], "YS": o["YS"], "QP": o["QP"], "BON": o["BON"], "GT": o["GT"], "PRED": pred,
                        "hp": hp, "wout": np.ascontiguousarray(L["w_out"]), "gpost": g8(L["norm_mix_post"]), "gffn": g8(L["norm_ffn_pre"]),
                        "gpost2": g8(L["norm_ffn_post"]), "wfi": np.ascontiguousarray(L["w_ff_in"].reshape(8, 128, 4096)),
                        "wfo": np.ascontiguousarray(L["w_ff_out"].reshape(32, 128, 1024))})
        r3 = run_bass_kernel_spmd(_prog("L3", NTl), in3, core_ids=list(range(ncores))).results
        X = [np.asarray(r3[ci]["XO"], np.float32) for ci in range(ncores)]
        _DBG.append(([dict(r) for r in r2], [x.copy() for x in X], [np.asarray(r3[ci]["YR"]) for ci in range(ncores)]))
    outs = []
    ci = 0
    for si, x in enumerate(seqs):
        n = x.shape[0] // NTl
        ys = [X[ci + s].reshape(1024, NTl).T for s in range(n)]
        ci += n
        outs.append(np.concatenate(ys, axis=0))
    return outs


def kernel_unfused(**inputs):
    xp = np.asarray(inputs["x_prompt"], np.float32)
    xs = np.asarray(inputs["x_sample"], np.float32)
    seqs = [xp[b] for b in range(xp.shape[0])] + [xs[b] for b in range(xs.shape[0])]
    outs = run_trunk(seqs, inputs, NT)
    nb = xp.shape[0]
    y_prompt = np.stack(outs[:nb], axis=0).astype(np.float32)
    y_sample = np.stack(outs[nb:], axis=0).astype(np.float32)
    return (y_prompt, y_sample)


def phase_pred(P, es_outer, HENDs, vflag_d, s, PRED, nseg):
    with ExitStack() as es:
        P.es = es
        identf = P.sb([128, 128], F32, "identf5"); identb = P.sb([128, 128], BF16, "identb5")
        make_ident(P, identf, identb)
        vf = P.sb([64, 2 * nseg * 3], F32, "vf")
        P.dma("sp", vf[:], vflag_d[:], reads=[vflag_d], writes=[vf])
        zero = P.sb([64, 128], F32, "zero5")
        P.o("pool", "memset", [], [zero], zero[:], 0.0)
        hep = rot([P.sb([64, 128], F32, f"he{i}") for i in range(3)])
        pdp = rot([P.sb([64, 128], F32, f"pd5{i}") for i in range(3)])
        pp = rot([P.ps([128, 512], F32, f"pp{i}") for i in range(2)])
        for d in range(2):
            js = list(range(s)) if d == 0 else list(range(nseg - 1, s, -1))
            off = 3 - len(js)
            for h in range(6):
                for slot in range(off):
                    P.dma("sp", PRED[d, h, slot, :, :], zero[:], reads=[zero], writes=[PRED])
                for k, j in enumerate(js):
                    he = hep(); pd = pdp(); p = pp()
                    fcol = (d * nseg + s) * 3 + k
                    P.dma("sp", he[:], HENDs[j][d, h, :, :], reads=[HENDs[j]], writes=[he])
                    P.o("pe", "matmul", [he, identf], [p], p[0:64, 0:64], lhsT=he[:, 64:128], rhs=identf[0:64, 0:64], start=True, stop=True)
                    P.o("dve", "tensor_scalar", [p, vf], [pd], out=pd[:, 0:64], in0=p[0:64, 0:64], scalar1=vf[:, fcol:fcol + 1], scalar2=0.0, op0=ALU.mult, op1=ALU.add)
                    P.o("pool", "tensor_scalar", [he, vf], [pd], out=pd[:, 64:128], in0=he[:, 0:64], scalar1=vf[:, fcol:fcol + 1], scalar2=0.0, op0=ALU.mult, op1=ALU.add)
                    P.dma("sp", PRED[d, h, off + k, :, :], pd[:], reads=[pd], writes=[PRED])
        P.phase_barrier()
    P.es = es_outer


NSEG = 4
W_NAMES = ("gain", "w_in", "poolw", "pscale", "invw", "convw", "hp", "dirp", "dup", "iup", "gup",
           "wout", "gpost", "gffn", "gpost2", "wfi", "wfo")
W_SHAPES = {"gain": [128, 8], "w_in": [8, 128, INW], "poolw": [2, 128, 128], "pscale": [128, 2], "invw": [128, 2],
            "convw": [64, 6, 3, 3], "hp": [64, 6, 5], "dirp": [64, 6, 2, 2], "dup": [2, 64, 384], "iup": [2, 64, 384], "gup": [128, 384],
            "wout": [1024, 1024], "gpost": [128, 8], "gffn": [128, 8], "gpost2": [128, 8], "wfi": [8, 128, 4096], "wfo": [32, 128, 1024]}


def build_fused(NTl, depth=2, nseg=NSEG):
    NEl = NTl + 2 * HL
    nc = bass.Bass("TRN2", target_bir_lowering=False)
    with ExitStack() as es:
        P = Prog(nc, es)
        I_ = lambda n, s, dt=F32: P.dram(n, s, dt, kind="ExternalInput")
        O_ = lambda n, s, dt=F32: P.dram(n, s, dt, kind="ExternalOutput")
        D_ = lambda n, s, dt=F32: P.dram(n, s, dt)
        def views(buf, n):
            return [Buf(buf.t[i], f"{buf.name}_{i}") for i in range(n)]
        X = [I_(f"x{s}", [8, 128, NTl]) for s in range(nseg)]
        Wl = [{n: I_(f"{n}_l{l}", W_SHAPES[n]) for n in W_NAMES} for l in range(depth)]
        abias = I_("abias", [128, 36, 128])
        maskA = I_("maskA", [128, 2, 128]); maskT = I_("maskT", [128, 2, 640])
        kmask = [I_(f"kmask{s}", [128, (NTl // 2048) * 96]) for s in range(nseg)]
        edge = [I_(f"edge{s}", [128, 2, 16]) for s in range(nseg)]
        hflag = [I_(f"hflag{s}", [128, 2]) for s in range(nseg)]
        kmask_o = I_("kmask_o", [128, (NTl // 2048) * 96]); edge_o = I_("edge_o", [128, 2, 16]); hflag_o = I_("hflag_o", [128, 2])
        vflag = I_("vflag", [64, 2 * nseg * 3])
        XOUT = O_("xo", [8, 128, NTl])
        XE_all = D_("XEall", [nseg, 8, 128, NEl]); XE = views(XE_all, nseg)
        XMID = [[D_(f"XM{l}_{s}", [8, 128, NTl]) for s in range(nseg)] for l in range(depth - 1)]
        PT = D_("PT", [22, 128, NEl]); Vt = D_("Vt", [NEl, 384], BF16)
        YP = [D_(f"YP{s}", [2, 128, NTl]) for s in range(nseg)]; YA = [D_(f"YA{s}", [6, 64, NTl]) for s in range(nseg)]
        YS_all = D_("YSall", [nseg, 6, 64, NTl]); YS = views(YS_all, nseg)
        QP_all = D_("QPall", [nseg, 2, 6, 64, NTl], BF16); QP = views(QP_all, nseg)
        BON_all = D_("BONall", [nseg, 6, 64, NTl]); BON = views(BON_all, nseg)
        GT_all = D_("GTall", [nseg, 6, 64, NTl]); GT = views(GT_all, nseg)
        HEND = [D_(f"HEND{s}", [2, 6, 64, 128]) for s in range(nseg)]
        PRED_all = D_("PREDall", [nseg, 2, 6, 3, 64, 128]); PREDv = views(PRED_all, nseg)
        PRED = D_("PRED", [2, 6, 3, 64, 128]); YR = D_("YR", [6, 64, NTl])
        XE_o = D_("XEo", [8, 128, NEl]); YS_o = D_("YSo", [6, 64, NTl]); QP_o = D_("QPo", [2, 6, 64, NTl], BF16)
        BON_o = D_("BONo", [6, 64, NTl]); GT_o = D_("GTo", [6, 64, NTl]); YP_o = D_("YPo", [2, 128, NTl]); YA_o = D_("YAo", [6, 64, NTl])

        def dyn_copy(dst, src_all, srcviews, pat_in, pat_out=None):
            def f(e):
                own = e.partition_id() % nseg
                src = src_all.t[tuple([bass.ds(own, 1)] + [slice(None)] * (len(src_all.t.shape) - 1))].rearrange(pat_in)
                d_ = dst.t if pat_out is None else dst.t.rearrange(pat_out)
                return e.dma_start(out=d_, in_=src)
            P.custom("sp", f, 16, reads=list(srcviews), writes=[dst])

        for l in range(depth):
            Wd = Wl[l]
            last = (l == depth - 1)
            src = X if l == 0 else XMID[l - 1]
            for s in range(nseg):
                ln = src[s - 1] if s > 0 else src[s]
                rn = src[s + 1] if s < nseg - 1 else src[s]
                P.dma("sp", XE[s][:, :, HL:HL + NTl], src[s][:], reads=[src[s]], writes=[XE[s]])
                P.dma("sp", XE[s][:, :, 0:HL], ln[:, :, NTl - HL:NTl], reads=[ln], writes=[XE[s]])
                P.dma("sp", XE[s][:, :, HL + NTl:NEl], rn[:, :, 0:HL], reads=[rn], writes=[XE[s]])
            P.phase_barrier()
            if not last:
                for s in range(nseg):
                    phase_A(P, es, XE[s], Wd["gain"], Wd["w_in"], PT, Vt, NEl // TS, hflag_d=hflag[s], far_cols=(16, 17, 18))
                    phase_pool(P, es, PT, Wd["poolw"], Wd["pscale"], Wd["invw"], edge[s], YP[s], NTl)
                    phase_att(P, es, PT, Vt, abias, kmask[s], YA[s], NTl)
                    phase_rwkv(P, es, PT, Wd["convw"], Wd["hp"], Wd["dirp"], Wd["dup"], Wd["iup"], Wd["gup"], maskA, maskT,
                               YS[s], QP[s], BON[s], GT[s], HEND[s], NTl)
                for s in range(nseg):
                    phase_pred(P, es, HEND, vflag, s, PRED, nseg)
                    phase_fin(P, es, YS[s], QP[s], BON[s], GT[s], PRED, Wd["hp"], YR, NTl)
                    phase_E(P, es, XE[s], YP[s], YR, YA[s], Wd["wout"], Wd["gpost"], Wd["gffn"], Wd["gpost2"], Wd["wfi"], Wd["wfo"], XMID[l][s], NTl, HL)
            else:
                nti = NEl // TS
                for s in range(nseg):
                    phase_A(P, es, XE[s], Wd["gain"], Wd["w_in"], PT, Vt, nti, hflag_d=hflag[s], cols=range(2, 13), do_v=False,
                            tiles=range(HL // TS - 1, nti - HL // TS + 1))
                    phase_rwkv(P, es, PT, Wd["convw"], Wd["hp"], Wd["dirp"], Wd["dup"], Wd["iup"], Wd["gup"], maskA, maskT,
                               YS[s], QP[s], BON[s], GT[s], HEND[s], NTl)
                    phase_pred(P, es, HEND, vflag, s, PREDv[s], nseg) if False else None
                for s in range(nseg):
                    phase_pred(P, es, HEND, vflag, s, PREDv[s], nseg)
                dyn_copy(XE_o, XE_all, XE, "o c p t -> (o c) p t")
                dyn_copy(YS_o, YS_all, YS, "o h p t -> (o h) p t")
                dyn_copy(QP_o, QP_all, QP, "o d h p t -> (o d) h p t")
                dyn_copy(BON_o, BON_all, BON, "o h p t -> (o h) p t")
                dyn_copy(GT_o, GT_all, GT, "o h p t -> (o h) p t")
                dyn_copy(PRED, PRED_all, PREDv, "o d h s p t -> (o d h s) p t", "d h s p t -> (d h s) p t")
                P.phase_barrier()
                phase_A(P, es, XE_o, Wd["gain"], Wd["w_in"], PT, Vt, nti, hflag_d=hflag_o, far_cols=(16, 17, 18))
                phase_pool(P, es, PT, Wd["poolw"], Wd["pscale"], Wd["invw"], edge_o, YP_o, NTl)
                phase_att(P, es, PT, Vt, abias, kmask_o, YA_o, NTl)
                phase_fin(P, es, YS_o, QP_o, BON_o, GT_o, PRED, Wd["hp"], YR, NTl)
                phase_E(P, es, XE_o, YP_o, YR, YA_o, Wd["wout"], Wd["gpost"], Wd["gffn"], Wd["gpost2"], Wd["wfi"], Wd["wfo"], XOUT, NTl, HL)
        P.barrier_all_dma("sp")
        P.emit()
    return nc


def fused_inputs(groups, inputs, NTl, nseg=NSEG, own=None):
    depth = inputs["w_in"].shape[0]
    abias = att_bias_tiles(np.asarray(inputs["rel_bias"], np.float32))
    wmaps = {}
    for l in range(depth):
        L = {k: np.asarray(inputs[k][l], np.float32) for k in LAYER_KEYS}
        convw, hp, dirp, maskA, maskT = rwkv_consts(L["rwkv_conv"], L["key_k"], L["key_a"], L["bonus_rk"], L["gn_gain"], L["gn_bias"], L["decay_w0"], L["iclr_a0"])
        bd, psc, iw, _ = pool_consts(L["pool_w"], L["pool_scale"], False, False, NTl)
        vals = {"gain": g8(L["norm_mix_pre"]), "w_in": np.ascontiguousarray(L["w_in"].reshape(8, 128, INW)), "poolw": bd, "pscale": psc, "invw": iw,
                "convw": convw, "hp": hp, "dirp": dirp, "dup": np.ascontiguousarray(L["decay_up"]), "iup": np.ascontiguousarray(L["iclr_up"]),
                "gup": np.ascontiguousarray(L["gate_up"]), "wout": np.ascontiguousarray(L["w_out"]), "gpost": g8(L["norm_mix_post"]),
                "gffn": g8(L["norm_ffn_pre"]), "gpost2": g8(L["norm_ffn_post"]), "wfi": np.ascontiguousarray(L["w_ff_in"].reshape(8, 128, 4096)),
                "wfo": np.ascontiguousarray(L["w_ff_out"].reshape(32, 128, 1024))}
        for n in W_NAMES:
            wmaps[f"{n}_l{l}"] = vals[n]
    NEl = NTl + 2 * HL
    in_maps = []
    for ci, seqs in enumerate(groups):
        m = dict(wmaps)
        m["abias"] = abias; m["maskA"] = maskA; m["maskT"] = maskT
        segs = []
        for x in seqs:
            n = x.shape[0] // NTl
            for q in range(n):
                segs.append((x, q, n))
        assert len(segs) == nseg
        cont = [0.0] * nseg
        for j, (x, q, n) in enumerate(segs):
            cont[j] = 1.0 if q > 0 else 0.0
        vflag = np.zeros((64, 2 * nseg * 3), np.float32)
        for s, (x, q, n) in enumerate(segs):
            m[f"x{s}"] = np.ascontiguousarray(x[q * NTl:(q + 1) * NTl].T.reshape(8, 128, NTl))
            valid = np.zeros(NEl, bool); valid[HL:HL + NTl] = True
            if q > 0:
                valid[0:HL] = True
            if q < n - 1:
                valid[HL + NTl:] = True
            m[f"kmask{s}"] = kmask_tiles(valid, NTl)
            _, _, _, ed = pool_consts(np.zeros((4, 64, 64), np.float32), np.zeros(256, np.float32), q == 0, q == n - 1, NTl)
            m[f"edge{s}"] = ed
            hf = np.zeros((128, 2), np.float32); hf[:, 0] = 1.0 if q > 0 else 0.0; hf[:, 1] = 1.0 if q < n - 1 else 0.0
            m[f"hflag{s}"] = hf
            for d in range(2):
                js = list(range(s)) if d == 0 else list(range(nseg - 1, s, -1))
                for k, j in enumerate(js):
                    if d == 0:
                        ok = all(cont[i] == 1.0 for i in range(j + 1, s + 1))
                    else:
                        ok = all(cont[i] == 1.0 for i in range(s + 1, j + 1))
                    vflag[:, (d * nseg + s) * 3 + k] = 1.0 if ok else 0.0
        m["vflag"] = vflag
        o = own[ci] if own is not None else 0
        m["kmask_o"] = m[f"kmask{o}"]; m["edge_o"] = m[f"edge{o}"]; m["hflag_o"] = m[f"hflag{o}"]
        in_maps.append(m)
    return in_maps


def kernel_fused(inputs, NTl=None, groups=None, own=None):
    NTl = NTl or NT
    xp = np.asarray(inputs["x_prompt"], np.float32)
    xs = np.asarray(inputs["x_sample"], np.float32)
    if groups is None:
        gp = [xp[b] for b in range(xp.shape[0])]
        gs = [xs[b] for b in range(xs.shape[0])]
        groups = [gp] * 4 + [gs] * 4
        own = [0, 1, 2, 3, 0, 1, 2, 3]
    key = ("F", NTl)
    if key not in _PROGS:
        _PROGS[key] = build_fused(NTl, depth=inputs["w_in"].shape[0])
    in_maps = fused_inputs(groups, inputs, NTl, own=own)
    res = run_bass_kernel_spmd(_PROGS[key], in_maps, core_ids=list(range(len(groups)))).results
    return [np.asarray(res[c]["xo"], np.float32).reshape(1024, NTl).T for c in range(len(groups))]


def kernel(**inputs):
    xp = np.asarray(inputs["x_prompt"], np.float32)
    xs = np.asarray(inputs["x_sample"], np.float32)
    outs = kernel_fused(inputs)
    y_prompt = np.stack([np.concatenate(outs[0:2], axis=0), np.concatenate(outs[2:4], axis=0)], axis=0).astype(np.float32)
    y_sample = np.concatenate(outs[4:8], axis=0)[None].astype(np.float32)
    return (y_prompt, y_sample)
```
